# Optimizing a Trainium2 kernel written in Bass

```python
import math, functools
import jax, jax.numpy as jnp
from jax import lax
import numpy as np

D_MODEL = 1024
BATCH = 8
SEQ = 8192
DEPTH = 1
DEC_BATCH = 32
DEC_SEQ = 16
PAST_LEN = 4096

CHUNK = 64
EPS = 1e-6
SSD_HEADS = 16
SSD_HEAD_DIM = 64
D_SSM = SSD_HEADS * SSD_HEAD_DIM
SSD_GROUPS = 2
D_STATE = 128
CONV_W = 4
D_CONV = D_SSM + 2 * SSD_GROUPS * D_STATE
FOX_HEADS = 8
FOX_HEAD_DIM = 64
D_FOX = FOX_HEADS * FOX_HEAD_DIM
Q_BLOCK = 128
D_MIX = D_SSM + D_FOX
SPLIT_SIZES = (D_SSM, D_CONV, SSD_HEADS, D_FOX, D_FOX, D_FOX, FOX_HEADS)
D_IN_PROJ = sum(SPLIT_SIZES)
D_FF = 4 * D_MODEL

kernel_name = "hybrid_ssd_fox_streaming_step"


def rms_norm(x, g):
    xf = x.astype(jnp.float32)
    y = xf * lax.rsqrt(jnp.mean(xf * xf, axis=-1, keepdims=True) + EPS)
    return (y * g.astype(jnp.float32)).astype(x.dtype)


def split_proj(u):
    idx = [int(i) for i in np.cumsum(SPLIT_SIZES)[:-1]]
    return jnp.split(u, idx, axis=-1)


def causal_conv(xbc, conv_prev, conv_w, conv_b):
    L = xbc.shape[1]
    xp = jnp.concatenate([conv_prev.astype(xbc.dtype), xbc], axis=1)
    out = conv_b + sum(xp[:, k:k + L] * conv_w[k] for k in range(CONV_W))
    return jax.nn.silu(out), xp[:, -(CONV_W - 1):]


def ssd_scan(x, dt, A, Bm, Cm, state0):
    b, L, H, P = x.shape
    cl = min(CHUNK, L)
    nc = L // cl
    R = H // SSD_GROUPS

    def chunks(t):
        return jnp.moveaxis(t.reshape((b, nc, cl) + t.shape[2:]), 1, 0)

    xs = chunks(x.reshape(b, L, SSD_GROUPS, R, P) * dt.reshape(b, L, SSD_GROUPS, R)[..., None])
    dA = chunks((dt * A).reshape(b, L, SSD_GROUPS, R))
    Bs, Cs = chunks(Bm), chunks(Cm)
    mask = jnp.tril(jnp.ones((cl, cl), dtype=bool))[None, :, :, None, None]

    def step(state, inp):
        xc, dAc, Bc, Cc = inp
        acs = jnp.cumsum(dAc, axis=1)
        seg = acs[:, :, None] - acs[:, None, :]
        decay = jnp.exp(jnp.where(mask, seg, -jnp.inf))
        cb = jnp.einsum("blgn,bsgn->blsg", Cc, Bc)
        y_diag = jnp.einsum("blsg,blsgr,bsgrp->blgrp", cb, decay, xc)
        y_off = jnp.einsum("blgn,bgrpn,blgr->blgrp", Cc, state, jnp.exp(acs))
        tail = jnp.exp(acs[:, -1:] - acs)
        new_state = state * jnp.exp(acs[:, -1])[..., None, None] + jnp.einsum(
            "bsgn,bsgr,bsgrp->bgrpn", Bc, tail, xc)
        return new_state, y_diag + y_off

    final, ys = lax.scan(step, state0.reshape(b, SSD_GROUPS, R, P, D_STATE), (xs, dA, Bs, Cs))
    y = jnp.moveaxis(ys, 0, 1).reshape(b, L, H, P)
    return y, final.reshape(b, H, P, D_STATE)


def ssd_mixer(z, xbc, dt_raw, conv_prev, ssm_prev, conv_w, conv_b, dt_bias, A_log, D_skip, ssd_norm_w):
    b, L, _ = z.shape
    f32 = jnp.float32
    xbc_c, conv_state = causal_conv(xbc, conv_prev, conv_w, conv_b)
    xs, Bm, Cm = jnp.split(xbc_c.astype(f32), [D_SSM, D_SSM + SSD_GROUPS * D_STATE], axis=-1)
    dt = jax.nn.softplus(dt_raw.astype(f32) + dt_bias.astype(f32))
    A = -jnp.exp(A_log.astype(f32))
    x_h = xs.reshape(b, L, SSD_HEADS, SSD_HEAD_DIM)
    y, ssm_state = ssd_scan(x_h, dt, A,
                            Bm.reshape(b, L, SSD_GROUPS, D_STATE),
                            Cm.reshape(b, L, SSD_GROUPS, D_STATE),
                            ssm_prev.astype(f32))
    y = y + x_h * D_skip.astype(f32)[:, None]
    y = y.reshape(b, L, D_SSM) * jax.nn.silu(z.astype(f32))
    yg = y.reshape(b, L, SSD_GROUPS, D_SSM // SSD_GROUPS)
    yg = yg * lax.rsqrt(jnp.mean(yg * yg, axis=-1, keepdims=True) + EPS)
    y = yg.reshape(b, L, D_SSM) * ssd_norm_w.astype(f32)
    return y.astype(z.dtype), ssm_state, conv_state


def fox_project(q, k, v, f_raw, f_bias, q_norm_w, k_norm_w):
    b, L, _ = q.shape
    shp = (b, L, FOX_HEADS, FOX_HEAD_DIM)
    q = rms_norm(q.reshape(shp), q_norm_w)
    k = rms_norm(k.reshape(shp), k_norm_w)
    v = v.reshape(shp)
    logf = jax.nn.log_sigmoid(f_raw.astype(jnp.float32) + f_bias.astype(jnp.float32))
    return q, k, v, logf


def fox_attend(q, k, v, cq, ck, q_pos, k_pos):
    f32 = jnp.float32
    s = jnp.einsum("bthd,bshd->bhts", q.astype(f32), k.astype(f32)) * (FOX_HEAD_DIM ** -0.5)
    bias = jnp.transpose(cq, (0, 2, 1))[..., :, None] - jnp.transpose(ck, (0, 2, 1))[..., None, :]
    allowed = k_pos[None, :] <= q_pos[:, None]
    s = jnp.where(allowed, s + bias, -jnp.inf)
    p = jax.nn.softmax(s, axis=-1)
    return jnp.einsum("bhts,bshd->bthd", p, v.astype(f32))


def fox_prompt(q, k, v, logf):
    b, L, H, d = q.shape
    c = jnp.cumsum(logf, axis=1)
    nb = L // Q_BLOCK
    qb = jnp.moveaxis(q.reshape(b, nb, Q_BLOCK, H, d), 1, 0)
    cb = jnp.moveaxis(c.reshape(b, nb, Q_BLOCK, H), 1, 0)
    pos = jnp.arange(L)
    pb = pos.reshape(nb, Q_BLOCK)
    out = lax.map(lambda a: fox_attend(a[0], k, v, a[1], c, a[2], pos), (qb, cb, pb))
    return jnp.moveaxis(out, 0, 1).reshape(b, L, H * d)


def fox_sample(q, k, v, logf, cache_k, cache_v, cache_logf):
    b, T, H, d = q.shape
    past = cache_k.shape[1]
    k_all = jnp.concatenate([cache_k.astype(jnp.float32), k.astype(jnp.float32)], axis=1)
    v_all = jnp.concatenate([cache_v.astype(jnp.float32), v.astype(jnp.float32)], axis=1)
    c_all = jnp.cumsum(jnp.concatenate([cache_logf.astype(jnp.float32), logf], axis=1), axis=1)
    q_pos = past + jnp.arange(T)
    k_pos = jnp.arange(past + T)
    out = fox_attend(q, k_all, v_all, c_all[:, past:], c_all, q_pos, k_pos)
    return out.reshape(b, T, H * d)


def trunk_layer(x, conv_prev, ssm_prev, fox_fn, norm1_w, w_in, conv_w, conv_b, dt_bias, A_log,
                D_skip, ssd_norm_w, f_bias, q_norm_w, k_norm_w, w_out, norm2_w, w_up, w_down):
    h = rms_norm(x, norm1_w)
    u = jnp.einsum("bld,de->ble", h, w_in)
    z, xbc, dt_raw, q, k, v, f_raw = split_proj(u)
    y_ssd, ssm_state, conv_state = ssd_mixer(z, xbc, dt_raw, conv_prev, ssm_prev, conv_w, conv_b,
                                             dt_bias, A_log, D_skip, ssd_norm_w)
    q, k, v, logf = fox_project(q, k, v, f_raw, f_bias, q_norm_w, k_norm_w)
    y_fox = fox_fn(q, k, v, logf).astype(x.dtype)
    mix = jnp.concatenate([y_ssd, y_fox], axis=-1)
    x = x + jnp.einsum("ble,ed->bld", mix, w_out)
    h = rms_norm(x, norm2_w)
    x = x + jnp.einsum("blf,fd->bld", jnp.square(jax.nn.relu(jnp.einsum("bld,df->blf", h, w_up))), w_down)
    return x, (k, v, logf, ssm_state, conv_state)


def setup_inputs(seed: int = 0) -> dict:
    key = jax.random.key(seed)
    ks = jax.random.split(key, 24)
    f32 = jnp.float32

    def nrm(k, shape, scale=1.0):
        return scale * jax.random.normal(k, shape, f32)

    x_prompt = nrm(ks[0], (BATCH, SEQ, D_MODEL))
    x_sample = nrm(ks[1], (DEC_BATCH, DEC_SEQ, D_MODEL))
    cache_k = nrm(ks[2], (DEPTH, DEC_BATCH, PAST_LEN, FOX_HEADS, FOX_HEAD_DIM))
    cache_v = nrm(ks[3], (DEPTH, DEC_BATCH, PAST_LEN, FOX_HEADS, FOX_HEAD_DIM))
    cache_logf = jax.nn.log_sigmoid(2.5 + nrm(ks[4], (DEPTH, DEC_BATCH, PAST_LEN, FOX_HEADS)))
    state_ssm = nrm(ks[5], (DEPTH, DEC_BATCH, SSD_HEADS, SSD_HEAD_DIM, D_STATE), 0.1)
    state_conv = nrm(ks[6], (DEPTH, DEC_BATCH, CONV_W - 1, D_CONV))
    norm1_w = 1.0 + nrm(ks[7], (DEPTH, D_MODEL), 0.01)
    w_in = nrm(ks[8], (DEPTH, D_MODEL, D_IN_PROJ), D_MODEL ** -0.5)
    conv_w = nrm(ks[9], (DEPTH, CONV_W, D_CONV), CONV_W ** -0.5)
    conv_b = nrm(ks[10], (DEPTH, D_CONV), 0.01)
    dt_init = jnp.exp(jax.random.uniform(ks[11], (DEPTH, SSD_HEADS), f32, math.log(1e-3), math.log(1e-1)))
    dt_bias = dt_init + jnp.log(-jnp.expm1(-dt_init))
    A_log = jnp.log(jax.random.uniform(ks[12], (DEPTH, SSD_HEADS), f32, 1.0, 16.0))
    D_skip = 1.0 + nrm(ks[13], (DEPTH, SSD_HEADS), 0.01)
    ssd_norm_w = 1.0 + nrm(ks[14], (DEPTH, D_SSM), 0.01)
    f_bias = jax.random.uniform(ks[15], (DEPTH, FOX_HEADS), f32, 1.0, 4.0)
    q_norm_w = 1.0 + nrm(ks[16], (DEPTH, FOX_HEAD_DIM), 0.01)
    k_norm_w = 1.0 + nrm(ks[17], (DEPTH, FOX_HEAD_DIM), 0.01)
    w_out = nrm(ks[18], (DEPTH, D_MIX, D_MODEL), D_MIX ** -0.5)
    norm2_w = 1.0 + nrm(ks[19], (DEPTH, D_MODEL), 0.01)
    w_up = nrm(ks[20], (DEPTH, D_MODEL, D_FF), D_MODEL ** -0.5)
    w_down = nrm(ks[21], (DEPTH, D_FF, D_MODEL), D_FF ** -0.5)
    return {"x_prompt": x_prompt, "x_sample": x_sample, "cache_k": cache_k, "cache_v": cache_v,
            "cache_logf": cache_logf, "state_ssm": state_ssm, "state_conv": state_conv,
            "norm1_w": norm1_w, "w_in": w_in, "conv_w": conv_w, "conv_b": conv_b, "dt_bias": dt_bias,
            "A_log": A_log, "D_skip": D_skip, "ssd_norm_w": ssd_norm_w, "f_bias": f_bias,
            "q_norm_w": q_norm_w, "k_norm_w": k_norm_w, "w_out": w_out, "norm2_w": norm2_w,
            "w_up": w_up, "w_down": w_down}


def reference(x_prompt, x_sample, cache_k, cache_v, cache_logf, state_ssm, state_conv,
              norm1_w, w_in, conv_w, conv_b, dt_bias, A_log, D_skip, ssd_norm_w, f_bias,
              q_norm_w, k_norm_w, w_out, norm2_w, w_up, w_down):
    b_p = x_prompt.shape[0]
    y_prompt, y_sample = x_prompt, x_sample
    p_states, s_states = [], []
    for l in range(DEPTH):
        wl = (norm1_w[l], w_in[l], conv_w[l], conv_b[l], dt_bias[l], A_log[l], D_skip[l],
              ssd_norm_w[l], f_bias[l], q_norm_w[l], k_norm_w[l], w_out[l], norm2_w[l], w_up[l], w_down[l])
        conv0 = jnp.zeros((b_p, CONV_W - 1, D_CONV), x_prompt.dtype)
        ssm0 = jnp.zeros((b_p, SSD_HEADS, SSD_HEAD_DIM, D_STATE), jnp.float32)
        y_prompt, st_p = trunk_layer(y_prompt, conv0, ssm0, fox_prompt, *wl)
        p_states.append(st_p)
        fox_fn = functools.partial(fox_sample, cache_k=cache_k[l], cache_v=cache_v[l], cache_logf=cache_logf[l])
        y_sample, st_s = trunk_layer(y_sample, state_conv[l], state_ssm[l], fox_fn, *wl)
        s_states.append(st_s)
    p_k = jnp.stack([s[0] for s in p_states])
    p_v = jnp.stack([s[1] for s in p_states])
    p_logf = jnp.stack([s[2] for s in p_states])
    p_ssm = jnp.stack([s[3] for s in p_states])
    p_conv = jnp.stack([s[4] for s in p_states])
    s_k = jnp.stack([s[0] for s in s_states])
    s_v = jnp.stack([s[1] for s in s_states])
    s_logf = jnp.stack([s[2] for s in s_states])
    s_ssm = jnp.stack([s[3] for s in s_states])
    s_conv = jnp.stack([s[4] for s in s_states])
    return (y_prompt, y_sample, p_k, p_v, p_logf, p_ssm, p_conv, s_k, s_v, s_logf, s_ssm, s_conv)
```

```python
import numpy as np
import concourse.bass as bass
import concourse.mybir as mybir
from concourse.bass_utils import run_bass_kernel_spmd

F32 = mybir.dt.float32
BF16 = mybir.dt.bfloat16
U8 = mybir.dt.uint8
AF = mybir.ActivationFunctionType
ALU = mybir.AluOpType
AX = mybir.AxisListType

_DT_SIZE = {F32: 4, BF16: 2, U8: 1}


class Buf:
    __slots__ = ("lw", "rd")

    def __init__(self):
        self.lw = None
        self.rd = {}


class Tile:
    def __init__(self, ap, nbufs=None):
        self._ap = ap
        self.buf = Buf()

    def ap(self):
        return self._ap


class _Rec:
    def __init__(self):
        self.call = None

    def __getattr__(self, name):
        def f(*a, **k):
            self.call = (name, a, k)
            return self
        return f


class Op:
    __slots__ = ("eng", "call", "inc", "reads", "writes", "busy", "lat", "preds", "nsucc", "succs", "seq", "idx", "dma_q")

    def __init__(self, eng, call, reads, writes, busy, lat, dma_q=None):
        self.eng = eng
        self.call = call
        self.reads = reads
        self.writes = writes
        self.busy = busy
        self.lat = lat
        self.dma_q = dma_q
        self.preds = ()
        self.succs = []
        self.seq = None


def _free_elems(ap):
    sh = ap.shape
    n = 1
    for d in sh[1:]:
        n *= int(d)
    return n


LAT_EXTRA = 60.0


class Prog:
    ENG = ("pe", "act", "dve", "pool", "sp")
    N_DMA_SEMS = {"sp": 28, "act": 4, "pool": 8}

    def __init__(self, nc, sbuf_bytes=None):
        self.nc = nc
        self.segments = [[]]
        self.sb_off = 0
        self.sb_stack = []
        self.ps_off = 0
        self.ps_stack = []
        self.sbuf_bytes = sbuf_bytes or 196608
        self.n_instr = 0

    def __enter__(self):
        from contextlib import ExitStack
        self.es = ExitStack()
        nc = self.nc
        self.sb_arena = self.es.enter_context(nc.sbuf_tensor("arena", [128, self.sbuf_bytes], U8))
        self.ps_arena = self.es.enter_context(nc.psum_tensor("psarena", [128, 16384], U8))
        self.sems = {}
        for e in ("pe", "act", "dve", "pool"):
            self.sems[e] = self.es.enter_context(nc.semaphore("s_" + e))
        for q, n in self.N_DMA_SEMS.items():
            for i in range(n):
                k = ("dma", q, i)
                self.sems[k] = self.es.enter_context(nc.semaphore("d_%s%d" % (q, i)))
        return self

    def __exit__(self, *a):
        return self.es.__exit__(*a)

    def sb(self, name, shape, dtype, parts=None):
        n = int(np.prod(shape[1:])) * _DT_SIZE[dtype]
        n_al = (n + 31) // 32 * 32
        off = self.sb_off
        assert off + n_al <= self.sbuf_bytes, "SBUF arena overflow at %s (%d)" % (name, off + n_al)
        self.sb_off += n_al
        self.sb_hwm = max(getattr(self, "sb_hwm", 0), self.sb_off)
        ap = self.sb_arena[0:shape[0], off:off + n].bitcast(dtype)
        ap = self._reshape(ap, shape)
        t_ = Tile(ap)
        t_.name = name
        return t_

    def ps(self, name, shape, dtype):
        n = int(np.prod(shape[1:])) * _DT_SIZE[dtype]
        n_al = (n + 2047) // 2048 * 2048
        off = self.ps_off
        assert off + n_al <= 16384, "PSUM overflow at %s" % name
        self.ps_off += n_al
        ap = self.ps_arena[0:shape[0], off:off + n].bitcast(dtype)
        ap = self._reshape(ap, shape)
        t = Tile(ap)
        t.name = name
        t.buf.rd = "psum"
        return t

    @staticmethod
    def _reshape(ap, shape):
        if len(shape) == 2:
            return ap
        names = "abcdefg"[:len(shape) - 1]
        kw = {names[i]: shape[i + 1] for i in range(len(shape) - 1)}
        return ap.rearrange("p (%s) -> p %s" % (" ".join(names), " ".join(names)), **kw)

    def push(self):
        self.sb_stack.append(self.sb_off)
        self.ps_stack.append(self.ps_off)

    def pop(self):
        self.barrier()
        self.sb_off = self.sb_stack.pop()
        self.ps_off = self.ps_stack.pop()

    def op(self, eng, fn, reads=(), writes=()):
        rec = _Rec()
        fn(rec)
        name, a, k = rec.call
        if eng == "pe":
            if name == "matmul":
                rhs = a[2] if len(a) > 2 else k["rhs"]
                n = _free_elems(rhs)
                mult = 4.0 if rhs.dtype == F32 else 1.0
                busy = mult * max(n, 64) / 2.0 + 16
            else:
                busy = 80.0
        else:
            out = a[0] if a else k.get("out", k.get("ap"))
            n = _free_elems(out)
            if name == "reciprocal":
                busy = (70 + 8 * n) / 0.96
            elif eng == "act":
                busy = (224 + n) / 1.2
            elif eng == "dve":
                busy = 380 + n / 0.96
            else:
                busy = 300 + n * 2.1
        o = Op(eng, rec.call, tuple(reads), tuple(writes), busy, busy + LAT_EXTRA)
        self.segments[-1].append(o)
        self.n_instr += 1

    def dma(self, q, out, in_, reads=(), writes=(), **kw):
        nbytes = _free_elems(out) * out.shape[0] * _DT_SIZE.get(out.dtype, 4)
        busy = 70.0 if q != "pool" else 900.0
        lat = 2200.0 + nbytes / 120.0
        o = Op(q, ("dma_start", (), dict(out=out, in_=in_, **kw)), tuple(reads), tuple(writes), busy, lat, dma_q=q)
        self.segments[-1].append(o)
        self.n_instr += 1

    def barrier(self):
        if self.segments[-1]:
            self.segments.append([])

    def make_identity(self, tile, dtype, n=128):
        ap = tile.ap()
        self.op("pool", lambda e: e.memset(ap, 1.0), writes=[tile])
        self.op("pool", lambda e: e.affine_select(out=ap, in_=ap, pattern=[[-1, n]], compare_op=ALU.is_equal,
                                                  fill=0.0, base=0, channel_multiplier=1), reads=[tile], writes=[tile])

    def _schedule_segment(self, ops, t0):
        import heapq
        lastw = {}
        readers = {}
        for i, o in enumerate(ops):
            o.idx = i
            preds = set()
            for t in o.reads:
                b = id(t.buf)
                w = lastw.get(b)
                if w is not None:
                    preds.add(w)
                if t.buf.rd == "psum":
                    r = readers.get(b)
                    if r:
                        preds.add(r[-1])
            for t in o.writes:
                b = id(t.buf)
                w = lastw.get(b)
                if w is not None:
                    preds.add(w)
                r = readers.get(b)
                if r:
                    preds.update(r)
            preds.discard(i)
            o.preds = tuple(preds)
            o.nsucc = 0
            o.succs = []
            for t in o.reads:
                readers.setdefault(id(t.buf), []).append(i)
            for t in o.writes:
                b = id(t.buf)
                lastw[b] = i
                readers[b] = []
        pending = [len(o.preds) for o in ops]
        for o in ops:
            for p in o.preds:
                ops[p].succs.append(o.idx)
        n = len(ops)
        blevel = [0.0] * n
        for i in range(n - 1, -1, -1):
            o = ops[i]
            m = 0.0
            for s_ in o.succs:
                if blevel[s_] > m:
                    m = blevel[s_]
            blevel[i] = o.lat + m
        avail = [t0] * n
        finish = [0.0] * n
        pend = {e: [] for e in self.ENG}
        rdy = {e: [] for e in self.ENG}
        efree = {e: t0 for e in self.ENG}
        for o in ops:
            if pending[o.idx] == 0:
                heapq.heappush(pend[o.eng], (t0, o.idx))
        order = []
        while len(order) < n:
            best = None
            for e in self.ENG:
                pe_, re_ = pend[e], rdy[e]
                while pe_ and pe_[0][0] <= efree[e]:
                    a_, i_ = heapq.heappop(pe_)
                    heapq.heappush(re_, (-blevel[i_], i_))
                if re_:
                    st = efree[e]
                elif pe_:
                    st = pe_[0][0]
                else:
                    continue
                if best is None or st < best[0]:
                    best = (st, e)
            assert best is not None, "cycle in DAG?"
            st, e = best
            if not rdy[e]:
                pe_ = pend[e]
                while pe_ and pe_[0][0] <= st:
                    a_, i_ = heapq.heappop(pe_)
                    heapq.heappush(rdy[e], (-blevel[i_], i_))
            _, i = heapq.heappop(rdy[e])
            o = ops[i]
            efree[e] = st + o.busy
            finish[i] = st + o.lat
            fin_same = st + o.busy + (0.0 if e == "pe" else 120.0)
            order.append(i)
            for s_ in o.succs:
                pending[s_] -= 1
                f_ = finish[i]
                if avail[s_] < f_:
                    avail[s_] = f_
                if pending[s_] == 0:
                    heapq.heappush(pend[ops[s_].eng], (avail[s_], s_))
        assert len(order) == len(ops), "cycle in DAG?"
        tend = max([t0] + [finish[i] for i in order])
        return order, tend

    def finish(self):
        nc = self.nc
        sems = self.sems
        count = {e: 0 for e in ("pe", "act", "dve", "pool")}
        dma_rr = {q: 0 for q in self.N_DMA_SEMS}
        dma_val = {k: 0 for k in sems if isinstance(k, tuple)}
        seen = {e: {} for e in self.ENG}
        queues = {e: [] for e in self.ENG}
        t = 0.0

        def full_barrier():
            cur = [(e, count[e]) for e in count if count[e] > 0]
            cur += [(k, v) for k, v in dma_val.items() if v > 0]
            for e in self.ENG:
                waits = []
                for k, v in cur:
                    if k == e and e == "pe":
                        continue
                    if seen[e].get(k, 0) < v:
                        seen[e][k] = v
                        waits.append((k, v))
                if waits:
                    queues[e].append((waits, None, None))

        for seg in self.segments:
            if not seg:
                continue
            order, t = self._schedule_segment(seg, t)
            for i in order:
                o = seg[i]
                e = o.eng
                waits = []
                sn = seen[e]
                for p in o.preds:
                    k, v = seg[p].seq
                    if k == "pe" and e == "pe":
                        continue
                    if sn.get(k, 0) >= v:
                        continue
                    sn[k] = v
                    waits.append((k, v))
                if o.dma_q is not None:
                    q = o.dma_q
                    j = dma_rr[q]
                    dma_rr[q] = (j + 1) % self.N_DMA_SEMS[q]
                    key = ("dma", q, j)
                    prev = dma_val[key]
                    if prev > 0 and sn.get(key, 0) < prev:
                        sn[key] = prev
                        waits.append((key, prev))
                    dma_val[key] = prev + 16
                    o.seq = (key, prev + 16)
                    queues[e].append((waits, o.call, (key, 16)))
                else:
                    count[e] += 1
                    o.seq = (e, count[e])
                    queues[e].append((waits, o.call, (e, 1)))
            full_barrier()
        self.est_ns = t

        def replay(name, eng):
            for waits, call, inc in queues[name]:
                for k, v in waits:
                    eng.wait_ge(sems[k], v)
                if call is not None:
                    name_, a_, k_ = call
                    ins = getattr(eng, name_)(*a_, **k_)
                    ins.then_inc(sems[inc[0]], inc[1])

        with nc.Block() as block:
            @block.tensor
            def _(e):
                replay("pe", e)

            @block.scalar
            def _(e):
                replay("act", e)

            @block.vector
            def _(e):
                replay("dve", e)

            @block.gpsimd
            def _(e):
                replay("pool", e)

            @block.sync
            def _(e):
                replay("sp", e)


D = 1024
DIN = 4120
C_Z, C_XBC, C_DT, C_Q, C_K, C_V, C_F = 0, 1024, 2560, 2576, 3088, 3600, 4112
EPS = 1e-6
NEG = -30000.0


class _Stop(Exception):
    pass


def build_program(S, PAST, NSTREAM=4, T=16):
    import os
    STOP = int(os.environ.get("K_STOP", "99"))
    NT1 = int(os.environ.get("K_NT1", "999"))
    SUB = int(os.environ.get("K_SUB", "99"))

    def chk(n):
        if SUB == n:
            raise _Stop()
    NT = S // 128
    NQG = S // 512
    NTT = NT + NSTREAM
    NJ = PAST // 128
    nc = bass.Bass("TRN2", target_bir_lowering=False)

    def din(name, shape):
        return nc.dram_tensor(name, list(shape), F32, kind="ExternalInput").ap()

    def dout(name, shape):
        return nc.dram_tensor(name, list(shape), F32, kind="ExternalOutput").ap()

    x_prompt = din("x_prompt", [S, D])
    x_sample = din("x_sample", [NSTREAM, T, D])
    cache_k = din("cache_k", [NSTREAM, PAST, 512])
    cache_v = din("cache_v", [NSTREAM, PAST, 512])
    cache_logf = din("cache_logf", [NSTREAM, PAST, 8])
    state_ssm = din("state_ssm", [NSTREAM, 1024, 128])
    state_conv = din("state_conv", [NSTREAM, 3, 1536])
    norm1_w = din("norm1_w", [D])
    w_in = din("w_in", [D, DIN])
    conv_w = din("conv_w", [4, 1536])
    conv_b = din("conv_b", [1536])
    dt_bias = din("dt_bias", [16])
    A_log = din("A_log", [16])
    D_skip = din("D_skip", [16])
    ssd_norm_w = din("ssd_norm_w", [D])
    f_bias = din("f_bias", [8])
    q_norm_w = din("q_norm_w", [64])
    k_norm_w = din("k_norm_w", [64])
    w_out = din("w_out", [1536, D])
    norm2_w = din("norm2_w", [D])
    w_up = din("w_up", [D, 4096])
    w_down = din("w_down", [4096, D])

    y_prompt = dout("y_prompt", [S, D])
    y_sample = dout("y_sample", [NSTREAM, T, D])
    p_k = dout("p_k", [S, 512])
    p_v = dout("p_v", [S, 512])
    p_logf = dout("p_logf", [S, 8])
    p_ssm = dout("p_ssm", [1024, 128])
    p_conv = dout("p_conv", [3, 1536])
    s_k = dout("s_k", [NSTREAM, T, 512])
    s_v = dout("s_v", [NSTREAM, T, 512])
    s_logf = dout("s_logf", [NSTREAM, T, 8])
    s_ssm = dout("s_ssm", [NSTREAM, 1024, 128])
    s_conv = dout("s_conv", [NSTREAM, 3, 1536])

    SK = "ExternalOutput" if os.environ.get("K_DBG") else "Internal"
    qT_s = nc.dram_tensor("qT_s", [8, 70, S], BF16, kind=SK).ap()
    kT_s = nc.dram_tensor("kT_s", [8, 70, S], BF16, kind=SK).ap()
    yn_s = nc.dram_tensor("yn_s", [NTT * 128, D], BF16, kind=SK).ap()
    yfT_s = nc.dram_tensor("yfT_s", [512, NTT * 128], BF16, kind=SK).ap()
    vS = nc.dram_tensor("vS", [8, NT, 128, 128], BF16, kind=SK).ap()
    wo_s = nc.dram_tensor("wo_s", [1536, D], BF16, kind="Internal").ap()
    wu_s = nc.dram_tensor("wu_s", [D, 4096], BF16, kind="Internal").ap()
    wdn_s = nc.dram_tensor("wdn_s", [4096, D], BF16, kind="Internal").ap()

    P = Prog(nc, sbuf_bytes=210944)
    with P:
      try:
        banks = [P.ps("bank%d" % i, [128, 512], F32) for i in range(8)]

        def bview(i, dtype, shape, parts=128, off=0):
            n = int(np.prod(shape))
            ap = banks[i].ap()[0:parts, :]
            if dtype != F32:
                ap = ap.bitcast(dtype)
            ap = ap[:, off:off + n]
            return Prog._reshape(ap, [parts] + list(shape))

        def op(eng, fn, reads=(), writes=()):
            P.op(eng, fn, reads, writes)

        identb = P.sb("identb", [128, 128], BF16)
        identf = P.sb("identf", [128, 128], F32)
        ones_f = P.sb("ones_f", [128, 128], F32)
        ones_b = P.sb("ones_b", [128, 512], BF16)
        LE_f = P.sb("LE_f", [128, 128], F32)
        GT_f = P.sb("GT_f", [128, 128], F32)
        GT_b = P.sb("GT_b", [128, 128], BF16)
        P.make_identity(identb, BF16)
        P.make_identity(identf, F32)
        op("pool", lambda e: e.memset(ones_f.ap(), 1.0), writes=[ones_f])
        op("pool", lambda e: e.memset(ones_b.ap(), 1.0), writes=[ones_b])
        op("pool", lambda e: e.memset(LE_f.ap(), 1.0), writes=[LE_f])
        op("pool", lambda e: e.affine_select(out=LE_f.ap(), in_=LE_f.ap(), pattern=[[1, 128]], compare_op=ALU.is_ge,
                                             fill=0.0, base=0, channel_multiplier=-1), reads=[LE_f], writes=[LE_f])
        op("pool", lambda e: e.memset(GT_f.ap(), 1.0), writes=[GT_f])
        op("pool", lambda e: e.affine_select(out=GT_f.ap(), in_=GT_f.ap(), pattern=[[-1, 128]], compare_op=ALU.is_gt,
                                             fill=0.0, base=0, channel_multiplier=1), reads=[GT_f], writes=[GT_f])

        op("pool", lambda e: e.tensor_copy(GT_b.ap(), GT_f.ap()), reads=[GT_f], writes=[GT_b])

        def bc_load(name, src, n):
            t = P.sb(name, [128, n], F32)
            P.dma("sp", t.ap(), src.partition_broadcast(128), writes=[t])
            return t

        dtb_b = bc_load("dtb_b", dt_bias, 16)
        negA_b = bc_load("negA_b", A_log, 16)
        D_b = bc_load("D_b", D_skip, 16)
        fb_b = bc_load("fb_b", f_bias, 8)
        qw_b = bc_load("qw_b", q_norm_w, 64)
        kw_b = bc_load("kw_b", k_norm_w, 64)
        op("act", lambda e: e.activation(out=negA_b.ap(), in_=negA_b.ap(), func=AF.Exp), reads=[negA_b], writes=[negA_b])
        op("dve", lambda e: e.tensor_scalar_mul(negA_b.ap(), negA_b.ap(), -1.0), reads=[negA_b], writes=[negA_b])
        op("dve", lambda e: e.tensor_scalar_mul(qw_b.ap(), qw_b.ap(), 0.125), reads=[qw_b], writes=[qw_b])
        negfb = P.sb("negfb", [8, 1], F32)
        P.dma("sp", negfb.ap(), f_bias.rearrange("(h o) -> h o", o=1), writes=[negfb])
        op("dve", lambda e: e.tensor_scalar_mul(negfb.ap(), negfb.ap(), -1.0), reads=[negfb], writes=[negfb])
        cw = P.sb("cw", [128, 12, 4], F32)
        cb = P.sb("cb", [128, 12], F32)
        for k in range(4):
            P.dma("sp", cw.ap()[:, :, k], conv_w[k].rearrange("(c p) -> p c", p=128), writes=[cw], allow_slow_non_contiguous=True)
        P.dma("sp", cb.ap(), conv_b.rearrange("(c p) -> p c", p=128), writes=[cb], allow_slow_non_contiguous=True)
        g1 = P.sb("g1", [128, 8], F32)
        g2 = P.sb("g2", [128, 8], F32)
        gn = P.sb("gn", [128, 8], F32)
        P.dma("sp", g1.ap(), norm1_w.rearrange("(c p) -> p c", p=128), writes=[g1], allow_slow_non_contiguous=True)
        P.dma("sp", g2.ap(), norm2_w.rearrange("(c p) -> p c", p=128), writes=[g2], allow_slow_non_contiguous=True)
        P.dma("sp", gn.ap(), ssd_norm_w.rearrange("(c p) -> p c", p=128), writes=[gn], allow_slow_non_contiguous=True)

        def alloc_weight(name, nk, ncols):
            return [P.sb("%s%d" % (name, kc), [128, ncols], BF16) for kc in range(nk)]

        def load_weight(tiles, src, nk, ncols, scale_t, scale_nk, stage):
            step = int(stage[0].ap().shape[1])
            i = 0
            for kc in range(nk):
                wt = tiles[kc]
                for c0 in range(0, ncols, step):
                    c1 = min(ncols, c0 + step)
                    st = stage[i % len(stage)]
                    i += 1
                    P.dma("sp", st.ap()[:, 0:c1 - c0], src[kc * 128:(kc + 1) * 128, c0:c1], writes=[st])
                    eng = ("act", "dve", "act", "pool", "act", "dve")[i % 6]
                    if scale_t is not None and kc < scale_nk and eng == "act":
                        op("act", lambda e, wt=wt, st=st, c0=c0, c1=c1, kc=kc: e.activation(
                            out=wt.ap()[:, c0:c1], in_=st.ap()[:, 0:c1 - c0], func=AF.Copy, scale=scale_t.ap()[:, kc:kc + 1]), reads=[st, scale_t], writes=[wt])
                    elif scale_t is not None and kc < scale_nk:
                        op(eng, lambda e, wt=wt, st=st, c0=c0, c1=c1, kc=kc: e.tensor_scalar_mul(
                            wt.ap()[:, c0:c1], st.ap()[:, 0:c1 - c0], scale_t.ap()[:, kc:kc + 1]), reads=[st, scale_t], writes=[wt])
                    else:
                        op(eng, lambda e, wt=wt, st=st, c0=c0, c1=c1: e.tensor_copy(
                            wt.ap()[:, c0:c1], st.ap()[:, 0:c1 - c0]), reads=[st], writes=[wt])

        seqs = []

        if STOP == 0:
            raise _Stop()
        P.push()
        vb_t = P.sb("vb_t", [128, 8, 128], BF16)
        op("pool", lambda e: e.memset(vb_t.ap(), 1.0), writes=[vb_t])
        s_qb = [P.sb("s_qb%d" % i, [128, 512], BF16) for i in range(NSTREAM)]
        s_kb = [P.sb("s_kb%d" % i, [128, 512], BF16) for i in range(NSTREAM)]
        s_vb = [P.sb("s_vb%d" % i, [128, 8, 65], BF16) for i in range(NSTREAM)]
        s_lf = [P.sb("s_lf%d" % i, [128, 8], F32) for i in range(NSTREAM)]
        for i in range(NSTREAM):
            op("pool", lambda e, i=i: e.memset(s_vb[i].ap(), 1.0), writes=[s_vb[i]])

        P.push()
        w_in_t = alloc_weight("w_in", 8, DIN)
        wd = P.sb("wd", [128, 12, 5, 128], BF16)
        for cc in range(12):
            for k in range(5):
                sc_ap = cw.ap()[:, cc, k:k + 1] if k < 4 else cb.ap()[:, cc:cc + 1]
                op("act", lambda e: e.activation(out=wd.ap()[:, cc, k, :], in_=identb.ap(), func=AF.Copy, scale=sc_ap), reads=[identb, cw, cb], writes=[wd])
        Dd = P.sb("Dd", [128, 16, 128], BF16)
        for h in range(16):
            op("act", lambda e: e.activation(out=Dd.ap()[:, h, :], in_=identb.ap(), func=AF.Copy, scale=D_b.ap()[:, h:h + 1]), reads=[identb, D_b], writes=[Dd])
        cst = P.sb("cst", [128, 12, 3], F32)
        cso = P.sb("cso", [128, 12, 3], F32)
        st_out = P.sb("st_out", [128, 8, 128], F32)
        for h in range(8):
            for c0 in range(0, S, 512):
                P.dma("sp", qT_s[h, 67:70, c0:c0 + 512], ones_b.ap()[0:3, :], reads=[ones_b])
                P.dma("sp", kT_s[h, 64:67, c0:c0 + 512], ones_b.ap()[0:3, :], reads=[ones_b])

        P.push()
        stage = [P.sb("stage%d" % i, [128, 2048], F32) for i in range(4)]
        load_weight(w_in_t, w_in, 8, DIN, g1, 8, stage)
        P.pop()

        ones8 = P.sb("ones8", [8, 128], F32)
        op("pool", lambda e: e.memset(ones8.ap(), 1.0), writes=[ones8])
        xt = [P.sb("xt%d" % i, [128, D], F32) for i in range(2)]
        junk = P.sb("junk", [128, D], BF16)
        st1 = P.sb("st1", [128, 8], F32)
        hb_2 = [P.sb("hb%d" % i, [128, D], BF16) for i in range(2)]
        hT_2 = [P.sb("hT%d" % i, [128, 8, 128], BF16) for i in range(2)]
        zs_2 = [P.sb("zs%d" % i_, [128, D], BF16) for i_ in range(2)]
        vtok = P.sb("vtok", [128, 512], F32)
        ssq = P.sb("ssq", [128, 8], F32)
        rs = P.sb("rs", [128, 8], F32)
        kn = P.sb("kn", [128, 512], F32)
        kn2 = P.sb("kn2", [128, 512], F32)
        sq = kn2
        qb = P.sb("qb", [128, 512], BF16)
        kb = P.sb("kb", [128, 512], BF16)
        qkT = P.sb("qkT", [64, 1, 8, 128], BF16)
        lft = P.sb("lft", [128, 8], F32)
        lfa = P.sb("lfa", [128, 8], F32)
        fT1 = P.sb("fT1", [8, 128], F32)
        fT2 = P.sb("fT2", [8, 128], F32)
        cT = P.sb("cT", [8, 128], F32)
        ccar = P.sb("ccar", [8, 1], F32)
        csp = P.sb("csp", [8, 6, 128], BF16)
        cr = P.sb("cr", [8, 128], F32)
        dtt = P.sb("dtt", [128, 16], F32)
        dt_ = P.sb("dt_", [128, 16], F32)
        dA = P.sb("dA", [128, 16], F32)
        xpre = [P.sb("xpre%d" % i, [128, 12, 131], BF16) for i in range(2)]
        xc_2 = [P.sb("xc%d" % i_, [128, 12, 128], BF16) for i_ in range(2)]
        xdt_2 = [P.sb("xdt%d" % i_, [128, D], BF16) for i_ in range(2)]
        xtok_2 = [P.sb("xtok%d" % i_, [128, D], BF16) for i_ in range(2)]
        xdtt_2 = [P.sb("xdtt%d" % i_, [128, D], BF16) for i_ in range(2)]
        Btok_2 = [P.sb("Btok%d" % i_, [128, 2, 128], BF16) for i_ in range(2)]
        cbm = P.sb("cbm", [128, 2, 128], F32)
        rseg_h = P.sb("rseg_h", [128, 16, 128], BF16)
        rseg_l = P.sb("rseg_l", [128, 16, 128], BF16)
        dAh = P.sb("dAh", [128, 16], BF16)
        dAl = P.sb("dAl", [128, 16], F32)
        decT = P.sb("decT", [128, 8, 128], BF16)
        MT_2 = [P.sb("MT%d" % i_, [128, 16, 128], BF16) for i_ in range(2)]
        eacs_2 = [P.sb("eacs%d" % i_, [128, 16], F32) for i_ in range(2)]
        tail = P.sb("tail", [128, 16], F32)
        cd_b_2 = [P.sb("cd_b%d" % i_, [128, 16], F32) for i_ in range(2)]
        dtail = P.sb("dtail", [128, 16], F32)
        t2 = P.sb("t2", [128, D], F32)
        gsb = P.sb("gsb", [128, D], F32)
        ssg = P.sb("ssg", [128, 4], F32)
        yn = P.sb("yn", [128, D], BF16)
        st_tmp = gsb

        class _View:
            def __init__(self, base, ap):
                self.buf = base.buf
                self._ap = ap

            def ap(self):
                return self._ap
        stT = P.sb("stT", [128, D], F32)
        stT_b = P.sb("stT_b", [128, D], BF16)

        def rstd_from(ssq_ap, out_ap, tmp_ap, n, tiles):
            op("act", lambda e: e.activation(out=tmp_ap, in_=ssq_ap, func=AF.Ln, scale=1.0 / n, bias=EPS), reads=tiles, writes=tiles)
            op("act", lambda e: e.activation(out=out_ap, in_=tmp_ap, func=AF.Exp, scale=-0.5), reads=tiles, writes=tiles)

        def load_x(ti, par):
            if ti < NT:
                P.dma("sp", xt[par].ap(), x_prompt[ti * 128:(ti + 1) * 128, :], writes=[xt[par]])
            elif ti < NTT:
                op("pool", lambda e: e.memset(xt[par].ap(), 0.0), writes=[xt[par]])
                P.dma("sp", xt[par].ap()[0:T, :], x_sample[ti - NT], writes=[xt[par]])

        def p1_vars(ti):
            par = ti % 2
            return (par, zs_2[par], xc_2[par], xdt_2[par], xtok_2[par], xdtt_2[par], Btok_2[par], MT_2[par], eacs_2[par], cd_b_2[par])

        def p1_A(ti):
            par, zs, xc, xdt, xtok, xdtt, Btok, MT, eacs, cd_b = p1_vars(ti)
            is_p = ti < NT
            L = 128 if is_p else T
            si = ti - NT
            X = xt[par]
            xp = xpre[par]
            xpo = xpre[1 - par]
            hb = hb_2[par]
            hT = hT_2[par]
            op("act", lambda e: e.activation(out=junk.ap(), in_=X.ap(), func=AF.Square, accum_out=st1.ap()[:, 0:1]), reads=[X], writes=[st1])
            rstd_from(st1.ap()[:, 0:1], st1.ap()[:, 2:3], st1.ap()[:, 1:2], D, [st1])
            op("act", lambda e: e.activation(out=hb.ap(), in_=X.ap(), func=AF.Copy, scale=st1.ap()[:, 2:3]), reads=[X, st1], writes=[hb])
            load_x(ti + 1, 1 - par)
            pT = bview(0, BF16, [8, 128])
            for c in range(8):
                op("pe", lambda e, c=c: e.transpose(pT[:, c, :], hb.ap()[:, c * 128:(c + 1) * 128], identb.ap()), reads=[hb, identb], writes=[banks[0]])
            op("dve", lambda e: e.tensor_copy(hT.ap(), pT), reads=[banks[0]], writes=[hT])

            def proj_tok(bank_i, c0, n, off=0):
                o = bview(bank_i, F32, [n], off=off)
                for kc in range(8):
                    op("pe", lambda e, kc=kc: e.matmul(o, hT.ap()[:, kc, :], w_in_t[kc].ap()[:, c0:c0 + n], start=(kc == 0), stop=(kc == 7)),
                       reads=[hT, w_in_t[kc]], writes=[banks[bank_i]])
                return o
            chk(1)
            def qk_norm(which, pp, bank_i):
                op("act", lambda e: e.activation(out=sq.ap(), in_=pp, func=AF.Square), reads=[banks[bank_i]], writes=[sq])
                op("dve", lambda e: e.tensor_reduce(out=ssq.ap(), in_=sq.ap().rearrange("p (h d) -> p h d", h=8), axis=AX.X, op=ALU.add), reads=[sq], writes=[ssq])
                rstd_from(ssq.ap(), rs.ap(), ssq.ap(), 64, [ssq, rs])
                op("dve", lambda e: e.tensor_tensor(kn.ap().rearrange("p (h d) -> p h d", h=8), pp.rearrange("p (h d) -> p h d", h=8),
                                                    rs.ap().unsqueeze(2).to_broadcast([128, 8, 64]), ALU.mult), reads=[banks[bank_i], rs], writes=[kn])
                if which == "k":
                    op("pool", lambda e: e.tensor_tensor(kn2.ap().rearrange("p (h d) -> p h d", h=8), kn.ap().rearrange("p (h d) -> p h d", h=8),
                                                         kw_b.ap().unsqueeze(1).to_broadcast([128, 8, 64]), ALU.mult), reads=[kn, kw_b], writes=[kn2])
                    kdst = kb if is_p else s_kb[si]
                    op("pool", lambda e: e.tensor_copy(kdst.ap(), kn2.ap()), reads=[kn2], writes=[kdst])
                    if is_p:
                        P.dma("sp", p_k[ti * 128:(ti + 1) * 128, :], kn2.ap(), reads=[kn2])
                    else:
                        P.dma("sp", s_k[si], kn2.ap()[0:T, :], reads=[kn2])
                else:
                    qdst = qb if is_p else s_qb[si]
                    op("pool", lambda e: e.tensor_tensor(qdst.ap().rearrange("p (h d) -> p h d", h=8), kn.ap().rearrange("p (h d) -> p h d", h=8),
                                                         qw_b.ap().unsqueeze(1).to_broadcast([128, 8, 64]), ALU.mult), reads=[kn, qw_b], writes=[qdst])

            z0 = proj_tok(1, C_Z, 512)
            z1 = proj_tok(2, C_Z + 512, 512)
            op("act", lambda e: e.activation(out=zs.ap()[:, 0:512], in_=z0, func=AF.Silu), reads=[banks[1]], writes=[zs])
            qp = proj_tok(1, C_Q, 512)
            op("act", lambda e: e.activation(out=zs.ap()[:, 512:1024], in_=z1, func=AF.Silu), reads=[banks[2]], writes=[zs])
            kp = proj_tok(2, C_K, 512)
            qk_norm("q", qp, 1)
            vp = proj_tok(1, C_V, 512)
            qk_norm("k", kp, 2)
            op("act", lambda e: e.activation(out=vtok.ap(), in_=vp, func=AF.Copy), reads=[banks[1]], writes=[vtok])
            vdst = vb_t if is_p else s_vb[si]
            op("pool", lambda e: e.tensor_copy(vdst.ap()[:, :, 0:64], vtok.ap().rearrange("p (h d) -> p h d", h=8)), reads=[vtok], writes=[vdst])
            if is_p:
                P.dma("sp", vS[:, ti, :, :].rearrange("h s d -> s h d"), vb_t.ap(), reads=[vb_t])
                P.dma("sp", p_v[ti * 128:(ti + 1) * 128, :], vtok.ap(), reads=[vtok])
            else:
                P.dma("sp", s_v[si], vtok.ap()[0:T, :], reads=[vtok])
            dtp = proj_tok(0, C_DT, 16, off=0)
            fp = proj_tok(0, C_F, 8, off=16)
            fTp = bview(0, F32, [128], parts=8, off=32)
            for kc in range(8):
                op("pe", lambda e, kc=kc: e.matmul(fTp, w_in_t[kc].ap()[:, C_F:C_F + 8], hT.ap()[:, kc, :], start=(kc == 0), stop=(kc == 7)),
                   reads=[hT, w_in_t[kc]], writes=[banks[0]])
            chk(2)
            if is_p:
                if ti == 0:
                    op("pool", lambda e: e.memset(xp.ap()[:, :, 0:3], 0.0), writes=[xp])
                else:
                    op("pool", lambda e: e.tensor_copy(xp.ap()[:, :, 0:3], xpo.ap()[:, :, 128:131]), reads=[xpo], writes=[xp])
            else:
                for k in range(3):
                    P.dma("sp", cst.ap()[:, :, k], state_conv[si, k].rearrange("(c p) -> p c", p=128), writes=[cst], allow_slow_non_contiguous=True)
                op("pool", lambda e: e.tensor_copy(xp.ap()[:, :, 0:3], cst.ap()), reads=[cst], writes=[xp])
            chk(5)
            op("dve", lambda e: e.tensor_tensor(lft.ap(), fp, fb_b.ap(), ALU.add), reads=[banks[0], fb_b], writes=[lft])
            op("act", lambda e: e.activation(out=lft.ap(), in_=lft.ap(), func=AF.Exp, scale=-1.0), reads=[lft], writes=[lft])
            op("act", lambda e: e.activation(out=lft.ap(), in_=lft.ap(), func=AF.Ln, bias=1.0), reads=[lft], writes=[lft])
            lfdst = lfa if is_p else s_lf[si]
            op("dve", lambda e: e.tensor_scalar_mul(lfdst.ap(), lft.ap(), -1.0), reads=[lft], writes=[lfdst])
            if is_p:
                P.dma("sp", p_logf[ti * 128:(ti + 1) * 128, :], lfdst.ap(), reads=[lfdst])
            else:
                P.dma("sp", s_logf[si], lfdst.ap()[0:T, :], reads=[lfdst])
            chk(6)
            if is_p:
                op("act", lambda e: e.activation(out=fT1.ap(), in_=fTp, func=AF.Exp, scale=-1.0, bias=negfb.ap()), reads=[banks[0], negfb], writes=[fT1])
                op("act", lambda e: e.activation(out=fT1.ap(), in_=fT1.ap(), func=AF.Ln, bias=1.0), reads=[fT1], writes=[fT1])
                op("dve", lambda e: e.tensor_scalar_mul(fT2.ap(), fT1.ap(), -1.0), reads=[fT1], writes=[fT2])
                if ti == 0:
                    op("dve", lambda e: e.tensor_tensor_scan(cT.ap(), ones8.ap(), fT2.ap(), 0.0, ALU.mult, ALU.add), reads=[ones8, fT2], writes=[cT])
                else:
                    op("dve", lambda e: e.tensor_tensor_scan(cT.ap(), ones8.ap(), fT2.ap(), ccar.ap(), ALU.mult, ALU.add), reads=[ones8, fT2, ccar], writes=[cT])
                op("dve", lambda e: e.tensor_copy(ccar.ap(), cT.ap()[:, 127:128]), reads=[cT], writes=[ccar])
                op("pool", lambda e: e.tensor_copy(csp.ap()[:, 0, :], cT.ap()), reads=[cT], writes=[csp])
                op("pool", lambda e: e.tensor_tensor(cr.ap(), cT.ap(), csp.ap()[:, 0, :], ALU.subtract), reads=[cT, csp], writes=[cr])
                op("pool", lambda e: e.tensor_copy(csp.ap()[:, 1, :], cr.ap()), reads=[cr], writes=[csp])
                op("pool", lambda e: e.tensor_tensor(cr.ap(), cr.ap(), csp.ap()[:, 1, :], ALU.subtract), reads=[cr, csp], writes=[cr])
                op("pool", lambda e: e.tensor_copy(csp.ap()[:, 2, :], cr.ap()), reads=[cr], writes=[csp])
                op("pool", lambda e: e.tensor_scalar_mul(csp.ap()[:, 3:6, :], csp.ap()[:, 0:3, :], -1.0), reads=[csp], writes=[csp])
                P.dma("sp", qT_s[:, 64:67, ti * 128:(ti + 1) * 128], csp.ap()[:, 0:3, :], reads=[csp])
                P.dma("sp", kT_s[:, 67:70, ti * 128:(ti + 1) * 128], csp.ap()[:, 3:6, :], reads=[csp])
                qkp = bview(2, BF16, [8, 128], parts=64)
                for src_t, which, dstT in ((qb, 0, qT_s), (kb, 1, kT_s)):
                    for h in range(8):
                        op("pe", lambda e, h=h, src_t=src_t: e.transpose(qkp[:, h, :], src_t.ap()[:, h * 64:(h + 1) * 64], identb.ap()),
                           reads=[src_t, identb], writes=[banks[2]])
                    op("dve", lambda e, which=which: e.tensor_copy(qkT.ap()[:, 0, :, :], qkp), reads=[banks[2]], writes=[qkT])
                    P.dma("sp", dstT[:, 0:64, ti * 128:(ti + 1) * 128].rearrange("h d t -> d h t"), qkT.ap()[:, 0, :, :], reads=[qkT])
            chk(7)
            op("dve", lambda e: e.tensor_tensor(dtt.ap(), dtp, dtb_b.ap(), ALU.add), reads=[banks[0], dtb_b], writes=[dtt])
            op("act", lambda e: e.activation(out=dtt.ap(), in_=dtt.ap(), func=AF.Exp), reads=[dtt], writes=[dtt])
            op("act", lambda e: e.activation(out=dt_.ap(), in_=dtt.ap(), func=AF.Ln, bias=1.0), reads=[dtt], writes=[dt_])
            op("dve", lambda e: e.tensor_tensor(dA.ap(), dt_.ap(), negA_b.ap(), ALU.mult), reads=[dt_, negA_b], writes=[dA])
            for grp in range(3):
                xbk = (3, 0, 3)[grp]
                xb = bview(xbk, F32, [4, 128])
                for j in range(4):
                    cc = grp * 4 + j
                    for kc in range(8):
                        op("pe", lambda e, kc=kc, cc=cc, j=j: e.matmul(xb[:, j, :], w_in_t[kc].ap()[:, C_XBC + cc * 128:C_XBC + (cc + 1) * 128],
                                                                        hT.ap()[:, kc, :], start=(kc == 0), stop=(kc == 7)),
                           reads=[hT, w_in_t[kc]], writes=[banks[xbk]])
                op("act", lambda e, grp=grp: e.activation(out=xp.ap()[:, grp * 4:(grp + 1) * 4, 3:131], in_=xb, func=AF.Copy), reads=[banks[xbk]], writes=[xp])
            chk(3)
            chk(8)
            for grp in range(3):
                cvk = (0, 3, 0)[grp]
                cvp = bview(cvk, F32, [4, 128])
                for j in range(4):
                    cc = grp * 4 + j
                    for k in range(5):
                        rhs_ap = xp.ap()[:, cc, k:k + 128] if k < 4 else ones_b.ap()[:, 0:128]
                        op("pe", lambda e: e.matmul(cvp[:, j, :], wd.ap()[:, cc, k, :], rhs_ap, start=(k == 0), stop=(k == 4)),
                           reads=[wd, xp, ones_b], writes=[banks[cvk]])
                op("act", lambda e: e.activation(out=xc.ap()[:, grp * 4:(grp + 1) * 4, :], in_=cvp, func=AF.Silu), reads=[banks[cvk]], writes=[xc])
            if ti == NT - 1:
                op("pool", lambda e: e.tensor_copy(cso.ap(), xp.ap()[:, :, 128:131]), reads=[xp], writes=[cso])
                for k in range(3):
                    P.dma("sp", p_conv[k].rearrange("(c p) -> p c", p=128), cso.ap()[:, :, k], reads=[cso], allow_slow_non_contiguous=True)
            if not is_p:
                op("pool", lambda e: e.tensor_copy(cso.ap(), xp.ap()[:, :, T:T + 3]), reads=[xp], writes=[cso])
                for k in range(3):
                    P.dma("sp", s_conv[si, k].rearrange("(c p) -> p c", p=128), cso.ap()[:, :, k], reads=[cso], allow_slow_non_contiguous=True)
            chk(9)
            xtp = bview(4, BF16, [16, 64])
            xtp2 = bview(4, BF16, [8, 128])
            for c in range(8):
                op("pe", lambda e, c=c: e.transpose(xtp2[:, c, :], xc.ap()[:, c, :], identb.ap()), reads=[xc, identb], writes=[banks[4]])
            chk(91)
            op("dve", lambda e: e.tensor_tensor(xdt.ap().rearrange("p (h d) -> p h d", h=16), xtp, dt_.ap().unsqueeze(2).to_broadcast([128, 16, 64]), ALU.mult),
               reads=[banks[4], dt_], writes=[xdt])
            chk(92)
            op("dve", lambda e: e.tensor_copy(xtok.ap().rearrange("p (c t) -> p c t", c=8), xtp2), reads=[banks[4]], writes=[xtok])
            chk(93)
            btp = bview(5, BF16, [2, 128], off=512)
            for g in range(2):
                op("pe", lambda e, g=g: e.transpose(btp[:, g, :], xc.ap()[:, 8 + g, :], identb.ap()), reads=[xc, identb], writes=[banks[5]])
            chk(94)
            op("dve", lambda e: e.tensor_copy(Btok.ap(), btp), reads=[banks[5]], writes=[Btok])
            chk(10)
            sm = bview(5, F32, [3, 16], off=384)
            op("pe", lambda e: e.matmul(sm[:, 0, :], LE_f.ap()[0:L, :], dA.ap()[0:L, :], start=True, stop=True), reads=[LE_f, dA], writes=[banks[5]])
            op("pe", lambda e: e.matmul(sm[:, 1, :], GT_f.ap()[0:L, :], dA.ap()[0:L, :], start=True, stop=True), reads=[GT_f, dA], writes=[banks[5]])
            op("pe", lambda e: e.matmul(sm[:, 2, :], ones_f.ap()[0:L, :], dA.ap()[0:L, :], start=True, stop=True), reads=[ones_f, dA], writes=[banks[5]])
            op("act", lambda e: e.activation(out=eacs.ap(), in_=sm[:, 0, :], func=AF.Exp), reads=[banks[5]], writes=[eacs])
            op("act", lambda e: e.activation(out=tail.ap(), in_=sm[:, 1, :], func=AF.Exp), reads=[banks[5]], writes=[tail])
            op("act", lambda e: e.activation(out=cd_b.ap(), in_=sm[:, 2, :], func=AF.Exp), reads=[banks[5]], writes=[cd_b])
            op("dve", lambda e: e.tensor_tensor(dtail.ap(), dt_.ap(), tail.ap(), ALU.mult), reads=[dt_, tail], writes=[dtail])
            op("dve", lambda e: e.tensor_tensor(xdtt.ap().rearrange("p (h d) -> p h d", h=16), xtp, dtail.ap().unsqueeze(2).to_broadcast([128, 16, 64]), ALU.mult),
               reads=[banks[4], dtail], writes=[xdtt])
            chk(11)
            cbp = bview(5, F32, [2, 128], off=0)
            for g in range(2):
                op("pe", lambda e, g=g: e.matmul(cbp[:, g, :], xc.ap()[:, 8 + g, :], xc.ap()[:, 10 + g, :], start=True, stop=True), reads=[xc], writes=[banks[5]])
            op("dve", lambda e: e.tensor_tensor(cbm.ap(), cbp, LE_f.ap().unsqueeze(1).to_broadcast([128, 2, 128]), ALU.mult), reads=[banks[5], LE_f], writes=[cbm])
            chk(12)
            op("pool", lambda e: e.tensor_copy(dAh.ap(), dA.ap()), reads=[dA], writes=[dAh])
            op("pool", lambda e: e.tensor_tensor(dAl.ap(), dA.ap(), dAh.ap(), ALU.subtract), reads=[dA, dAh], writes=[dAl])
            op("pool", lambda e: e.tensor_tensor(rseg_h.ap(), LE_f.ap().unsqueeze(1).to_broadcast([128, 16, 128]), dAh.ap().unsqueeze(2).to_broadcast([128, 16, 128]), ALU.mult),
               reads=[LE_f, dAh], writes=[rseg_h])
            op("dve", lambda e: e.tensor_tensor(rseg_l.ap(), LE_f.ap().unsqueeze(1).to_broadcast([128, 16, 128]), dAl.ap().unsqueeze(2).to_broadcast([128, 16, 128]), ALU.mult),
               reads=[LE_f, dAl], writes=[rseg_l])
            for g in range(2):
                for q4 in range(2):
                    sg = bview(6 + q4, F32, [4, 128])
                    op("pe", lambda e, g=g, q4=q4, sg=sg: e.matmul(sg, GT_b.ap()[0:L, :], rseg_h.ap()[0:L, g * 8 + q4 * 4:g * 8 + q4 * 4 + 4, :], start=True, stop=False),
                       reads=[GT_b, rseg_h], writes=[banks[6 + q4]])
                    op("pe", lambda e, g=g, q4=q4, sg=sg: e.matmul(sg, GT_b.ap()[0:L, :], rseg_l.ap()[0:L, g * 8 + q4 * 4:g * 8 + q4 * 4 + 4, :], start=False, stop=True),
                       reads=[GT_b, rseg_l], writes=[banks[6 + q4]])
                    op("act", lambda e, q4=q4, sg=sg: e.activation(out=decT.ap()[:, q4 * 4:(q4 + 1) * 4, :], in_=sg, func=AF.Exp), reads=[banks[6 + q4]], writes=[decT])
                op("dve" if g == 0 else "pool", lambda e, g=g: e.tensor_tensor(MT.ap()[:, g * 8:(g + 1) * 8, :], decT.ap(), cbm.ap()[:, g:g + 1, :].to_broadcast([128, 8, 128]), ALU.mult),
                   reads=[decT, cbm], writes=[MT])

        def p1_B(ti):
            par, zs, xc, xdt, xtok, xdtt, Btok, MT, eacs, cd_b = p1_vars(ti)
            is_p = ti < NT
            L = 128 if is_p else T
            si = ti - NT
            if ti == 0:
                op("pool", lambda e: e.memset(stT.ap(), 0.0), writes=[stT])
                op("pool", lambda e: e.memset(stT_b.ap(), 0.0), writes=[stT_b])
            if not is_p:
                P.dma("sp", st_out.ap(), state_ssm[si].rearrange("(c p) n -> p c n", p=128), writes=[st_out])
                for half in range(2):
                    stp = bview(4 + half, F32, [4, 128])
                    for j in range(4):
                        op("pe", lambda e, j=j, half=half, stp=stp: e.transpose(stp[:, j, :], st_out.ap()[:, half * 4 + j, :], identf.ap()),
                           reads=[st_out, identf], writes=[banks[4 + half]])
                    op("dve", lambda e, half=half, stp=stp: e.tensor_copy(stT.ap()[:, half * 512:(half + 1) * 512], stp.rearrange("p a b -> p (a b)")),
                       reads=[banks[4 + half]], writes=[stT])
                op("act", lambda e: e.activation(out=stT_b.ap(), in_=stT.ap(), func=AF.Copy), reads=[stT], writes=[stT_b])
            chk(14)
            yb = [bview(6, F32, [8, 64]), bview(7, F32, [8, 64])]
            for h in range(16):
                op("pe", lambda e, h=h: e.matmul(yb[h // 8][:, h % 8, :], MT.ap()[0:L, h, :], xdt.ap()[0:L, h * 64:(h + 1) * 64], start=True, stop=False),
                   reads=[MT, xdt], writes=[banks[6 + h // 8]])
                op("pe", lambda e, h=h: e.matmul(yb[h // 8][:, h % 8, :], Dd.ap()[0:L, h, :], xtok.ap()[0:L, h * 64:(h + 1) * 64], start=False, stop=True),
                   reads=[Dd, xtok], writes=[banks[6 + h // 8]])
            for g in range(2):
                tb = bview(4 + g, F32, [512])
                op("pe", lambda e, g=g, tb=tb: e.matmul(tb, xc.ap()[:, 10 + g, :], stT_b.ap()[:, g * 512:(g + 1) * 512], start=True, stop=True),
                   reads=[xc, stT_b], writes=[banks[4 + g]])
                op("dve", lambda e, g=g, tb=tb: e.tensor_tensor(t2.ap()[:, g * 512:(g + 1) * 512].rearrange("p (h d) -> p h d", h=8), tb.rearrange("p (h d) -> p h d", h=8),
                                                                 eacs.ap()[:, g * 8:(g + 1) * 8].unsqueeze(2).to_broadcast([128, 8, 64]), ALU.mult),
                   reads=[banks[4 + g], eacs], writes=[t2])
            chk(15)
            for g in range(2):
                ib = bview(4 + g, F32, [512])
                op("pe", lambda e, g=g, ib=ib: e.matmul(ib, Btok.ap()[0:L, g, :], xdtt.ap()[0:L, g * 512:(g + 1) * 512], start=True, stop=True),
                   reads=[Btok, xdtt], writes=[banks[4 + g]])
            op("pool", lambda e: e.tensor_tensor(st_tmp.ap().rearrange("p (h d) -> p h d", h=16), stT.ap().rearrange("p (h d) -> p h d", h=16),
                                                 cd_b.ap().unsqueeze(2).to_broadcast([128, 16, 64]), ALU.mult), reads=[stT, cd_b], writes=[st_tmp])
            for g in range(2):
                ib = bview(4 + g, F32, [512])
                op("dve", lambda e, g=g, ib=ib: e.tensor_tensor(stT.ap()[:, g * 512:(g + 1) * 512], st_tmp.ap()[:, g * 512:(g + 1) * 512], ib, ALU.add),
                   reads=[st_tmp, banks[4 + g]], writes=[stT])
            op("act", lambda e: e.activation(out=stT_b.ap(), in_=stT.ap(), func=AF.Copy), reads=[stT], writes=[stT_b])
            if ti == NT - 1 or not is_p:
                for half in range(2):
                    stp = bview(4 + half, F32, [4, 128])
                    for j in range(4):
                        c = half * 4 + j
                        op("pe", lambda e, j=j, c=c, stp=stp: e.transpose(stp[:, j, :], stT.ap()[:, c * 128:(c + 1) * 128], identf.ap()),
                           reads=[stT, identf], writes=[banks[4 + half]])
                    op("dve", lambda e, half=half, stp=stp: e.tensor_copy(st_out.ap()[:, half * 4:(half + 1) * 4, :], stp), reads=[banks[4 + half]], writes=[st_out])
                dst = p_ssm if is_p else s_ssm[si]
                P.dma("sp", dst.rearrange("(c p) n -> p c n", p=128), st_out.ap(), reads=[st_out])
            chk(16)
            for g in range(2):
                yv = bview(6 + g, F32, [512])
                op("dve", lambda e, g=g, yv=yv: e.tensor_tensor(t2.ap()[:, g * 512:(g + 1) * 512], yv, t2.ap()[:, g * 512:(g + 1) * 512], ALU.add),
                   reads=[banks[6 + g], t2], writes=[t2])
            op("pool", lambda e: e.tensor_tensor(gsb.ap(), t2.ap(), zs.ap(), ALU.mult), reads=[t2, zs], writes=[gsb])
            for g in range(2):
                op("act", lambda e, g=g: e.activation(out=junk.ap()[:, g * 512:(g + 1) * 512], in_=gsb.ap()[:, g * 512:(g + 1) * 512], func=AF.Square,
                                                       accum_out=ssg.ap()[:, g:g + 1]), reads=[gsb], writes=[ssg])
            rstd_from(ssg.ap()[:, 0:2], ssg.ap()[:, 2:4], ssg.ap()[:, 0:2], 512, [ssg])
            for g in range(2):
                op("act", lambda e, g=g: e.activation(out=yn.ap()[:, g * 512:(g + 1) * 512], in_=gsb.ap()[:, g * 512:(g + 1) * 512], func=AF.Copy,
                                                       scale=ssg.ap()[:, 2 + g:3 + g]), reads=[gsb, ssg], writes=[yn])
            P.dma("sp", yn_s[ti * 128:(ti + 1) * 128, :], yn.ap(), reads=[yn])

        if STOP == 1:
            raise _Stop()
        load_x(0, 0)
        p1_A(0)
        for ti in range(NTT):
            if ti + 1 < NTT:
                p1_A(ti + 1)
            p1_B(ti)
        P.pop()
        if STOP == 2:
            raise _Stop()

        P.push()
        maskb = P.sb("maskb", [128, 4, 512], BF16)
        op("pool", lambda e: e.memset(maskb.ap(), 0.0), writes=[maskb])
        for j in range(4):
            op("pool", lambda e, j=j: e.affine_select(out=maskb.ap()[:, j, :], in_=maskb.ap()[:, j, :], pattern=[[1, 512]], compare_op=ALU.is_ge,
                                                      fill=NEG, base=-128 * j, channel_multiplier=-1), reads=[maskb], writes=[maskb])
        Vh = [P.sb("Vh%d" % i, [128, NT, 128], BF16) for i in range(2)]
        KT = [P.sb("KT%d" % i, [70, S], BF16) for i in range(2)]
        QT = [P.sb("QT%d" % i, [70, S], BF16) for i in range(2)]
        PT = [P.sb("PT%d" % i, [128, 2, 512], BF16) for i in range(4)]
        osb = [P.sb("osb%d" % i, [128, 512], F32) for i in range(2)]
        recb = [P.sb("recb%d" % i, [64, 512], F32) for i in range(2)]
        yf = [P.sb("yf%d" % i, [64, 512], BF16) for i in range(2)]

        def load_head(h):
            hp = h % 2
            P.dma("sp", Vh[hp].ap(), vS[h].rearrange("b s d -> s b d"), writes=[Vh[hp]])
            for c0 in range(0, S, 2048):
                c1 = min(S, c0 + 2048)
                P.dma("sp", KT[hp].ap()[:, c0:c1], kT_s[h, :, c0:c1], writes=[KT[hp]])
                P.dma("sp", QT[hp].ap()[:, c0:c1], qT_s[h, :, c0:c1], writes=[QT[hp]])

        jobs = []
        gidx = 0
        for h in range(8):
            for qg in range(NQG):
                nkb = 4 * qg + 4
                for kb0 in range(0, nkb, 2):
                    jobs.append((h, qg, kb0, nkb, gidx))
                gidx += 1
        loaded = set()

        def ensure_head(h):
            if h < 8 and h not in loaded:
                loaded.add(h)
                load_head(h)

        def emit_qk(ji):
            h, qg, kb0, nkb, g = jobs[ji]
            hp = h % 2
            sp_ = ji % 2
            for j in range(2):
                kbi = kb0 + j
                sc = bview(2 * sp_ + j, F32, [512])
                diag = kbi >= 4 * qg
                c0_ = 128 * (kbi - 4 * qg) if diag else 0
                op("pe", lambda e: e.matmul(sc[:, c0_:512], KT[hp].ap()[:, kbi * 128:(kbi + 1) * 128], QT[hp].ap()[:, qg * 512 + c0_:(qg + 1) * 512],
                                            start=True, stop=(not diag)), reads=[KT[hp], QT[hp]], writes=[banks[2 * sp_ + j]])
                if diag:
                    jj = kbi - 4 * qg
                    op("pe", lambda e: e.matmul(sc[:, c0_:512], identb.ap(), maskb.ap()[:, jj, c0_:512], start=False, stop=True),
                       reads=[identb, maskb], writes=[banks[2 * sp_ + j]])

        def emit_exp_pv(ji):
            h, qg, kb0, nkb, g = jobs[ji]
            hp = h % 2
            sp_ = ji % 2
            oi = 4
            oacc = bview(oi, F32, [512], parts=128)
            sc2 = P.ps_arena[0:128, (2 * sp_) * 2048:(2 * sp_ + 2) * 2048].bitcast(F32).rearrange("p (a b) -> p a b", a=2)
            op("act", lambda e: e.activation(out=PT[ji % 4].ap(), in_=sc2, func=AF.Exp),
               reads=[banks[2 * sp_], banks[2 * sp_ + 1]], writes=[PT[ji % 4]])
            for j in range(2):
                kbi = kb0 + j
                c0_ = 128 * (kbi - 4 * qg) if kbi >= 4 * qg else 0
                op("pe", lambda e: e.matmul(oacc[:, c0_:512], Vh[hp].ap()[:, kbi, :], PT[ji % 4].ap()[:, j, c0_:512], start=(kbi == 0), stop=(kbi == nkb - 1)),
                   reads=[Vh[hp], PT[ji % 4]], writes=[banks[oi]])

        def emit_epi1(ji):
            h, qg, kb0, nkb, g = jobs[ji]
            oi = 4
            op_ = g % 2
            oacc = bview(oi, F32, [512], parts=128)
            op("dve", lambda e: e.tensor_copy(osb[op_].ap(), oacc), reads=[banks[oi]], writes=[osb[op_]])
            op("pool", lambda e: e.tensor_copy(recb[op_].ap(), osb[op_].ap()[64:128, :]), reads=[osb[op_]], writes=[recb[op_]])
            op("dve", lambda e: e.reciprocal(recb[op_].ap(), recb[op_].ap()), reads=[recb[op_]], writes=[recb[op_]])
            op("pool", lambda e: e.tensor_tensor(yf[op_].ap(), osb[op_].ap()[0:64, :], recb[op_].ap(), ALU.mult), reads=[osb[op_], recb[op_]], writes=[yf[op_]])
            P.dma("sp", yfT_s[h * 64:(h + 1) * 64, qg * 512:(qg + 1) * 512], yf[op_].ap(), reads=[yf[op_]])

        def emit_epi2(ji):
            pass

        ensure_head(0)
        emit_qk(0)
        pending = None
        for ji in range(len(jobs)):
            h, qg, kb0, nkb, g = jobs[ji]
            if kb0 == 0 and qg == 0:
                ensure_head(h + 1)
            if ji + 1 < len(jobs):
                emit_qk(ji + 1)
            emit_exp_pv(ji)
            if pending is not None:
                emit_epi2(pending)
                pending = None
            if kb0 + 2 >= nkb:
                emit_epi1(ji)
                pending = ji
        if pending is not None:
            emit_epi2(pending)
        wst = [P.sb("wst%d" % i, [128, 512], F32) for i in range(2)]
        wbf = [P.sb("wbf%d" % i, [128, 512], BF16) for i in range(2)]
        wi_ = 0
        for src, dst, nk, ncols, sc_t, sc_nk in ((w_out, wo_s, 12, D, gn, 8), (w_up, wu_s, 8, 4096, g2, 8), (w_down, wdn_s, 32, D, None, 0)):
            for kc in range(nk):
                for c0 in range(0, ncols, 512):
                    bi = wi_ % 2
                    eng_ = "pool" if wi_ % 3 == 2 else "dve"
                    wi_ += 1
                    P.dma("sp", wst[bi].ap(), src[kc * 128:(kc + 1) * 128, c0:c0 + 512], writes=[wst[bi]])
                    if sc_t is not None and kc < sc_nk:
                        op(eng_, lambda e: e.tensor_scalar_mul(wbf[bi].ap(), wst[bi].ap(), sc_t.ap()[:, kc:kc + 1]), reads=[wst[bi], sc_t], writes=[wbf[bi]])
                    else:
                        op(eng_, lambda e: e.tensor_copy(wbf[bi].ap(), wst[bi].ap()), reads=[wst[bi]], writes=[wbf[bi]])
                    P.dma("sp", dst[kc * 128:(kc + 1) * 128, c0:c0 + 512], wbf[bi].ap(), reads=[wbf[bi]])
        if STOP == 3:
            raise _Stop()
        G = 4 if NJ % 4 == 0 else (2 if NJ % 2 == 0 else 1)
        lf = P.sb("lf", [128, NJ, 8], F32)
        Wt = P.sb("Wt", [128, NJ, 8], F32)
        Tt = P.sb("Tt", [128, 8], F32)
        nb = P.sb("nb", [128, NJ, 8], F32)
        nbn = P.sb("nbn", [16, 8], F32)
        cmask = P.sb("cmask", [16, 16], F32)
        Qbd = P.sb("Qbd", [128, 4, 32], BF16)
        kc_f = [P.sb("kc_f%d" % i, [128, G, 512], F32) for i in range(2)]
        vc_f = [P.sb("vc_f%d" % i, [128, G, 512], F32) for i in range(2)]
        kc_b = [P.sb("kc_b%d" % i, [128, G, 512], BF16) for i in range(2)]
        vc_b = [P.sb("vc_b%d" % i, [128, G, 8, 65], BF16) for i in range(2)]
        k2T = [P.sb("k2T%d" % i, [128, 4, 128], BF16) for i in range(2)]
        sbs = [P.sb("sbs%d" % i, [128, 8, 16], F32) for i in range(2)]
        pTs = [P.sb("pTs%d" % i, [128, 8, 16], BF16) for i in range(2)]
        osb_s = P.sb("osb_s", [16, 8, 65], F32)
        rec_s = P.sb("rec_s", [16, 8, 1], F32)
        yf_s = P.sb("yf_s", [16, 512], BF16)
        yfT_sb = P.sb("yfT_sb", [128, 4, 16], BF16)
        for i in range(2):
            op("pool", lambda e, i=i: e.memset(vc_b[i].ap(), 1.0), writes=[vc_b[i]])
        op("pool", lambda e: e.memset(cmask.ap(), 0.0), writes=[cmask])
        op("pool", lambda e: e.affine_select(out=cmask.ap(), in_=cmask.ap(), pattern=[[1, 16]], compare_op=ALU.is_ge, fill=NEG, base=0, channel_multiplier=-1),
           reads=[cmask], writes=[cmask])

        def load_cache(st, g0, par):
            P.dma("sp", kc_f[par].ap(), cache_k[st].rearrange("(p j) c -> p j c", j=NJ)[:, g0:g0 + G, :], writes=[kc_f[par]])
            P.dma("sp", vc_f[par].ap(), cache_v[st].rearrange("(p j) c -> p j c", j=NJ)[:, g0:g0 + G, :], writes=[vc_f[par]])

        gi = 0
        for st in range(NSTREAM):
            load_cache(st, 0, gi % 2)
            P.dma("sp", lf.ap(), cache_logf[st].rearrange("(p j) h -> p j h", j=NJ), writes=[lf])
            op("pool", lambda e: e.memset(Wt.ap()[:, NJ - 1, :], 0.0), writes=[Wt])
            for j in range(NJ - 2, -1, -1):
                op("pool", lambda e, j=j: e.tensor_tensor(Wt.ap()[:, j, :], Wt.ap()[:, j + 1, :], lf.ap()[:, j + 1, :], ALU.add), reads=[Wt, lf], writes=[Wt])
            op("pool", lambda e: e.tensor_tensor(Tt.ap(), Wt.ap()[:, 0, :], lf.ap()[:, 0, :], ALU.add), reads=[Wt, lf], writes=[Tt])
            xps = bview(6, F32, [8], off=384)
            op("pe", lambda e: e.matmul(xps, GT_f.ap(), Tt.ap(), start=True, stop=True), reads=[GT_f, Tt], writes=[banks[6]])
            op("dve", lambda e: e.tensor_tensor(nb.ap(), Wt.ap(), xps.unsqueeze(1).to_broadcast([128, NJ, 8]), ALU.add), reads=[Wt, banks[6]], writes=[nb])
            dps = bview(6, F32, [8], parts=16, off=392)
            op("pe", lambda e: e.matmul(dps, LE_f.ap()[0:16, 0:16], s_lf[st].ap()[0:16, :], start=True, stop=True), reads=[LE_f, s_lf[st]], writes=[banks[6]])
            op("dve", lambda e: e.tensor_scalar_mul(nbn.ap(), dps, -1.0), reads=[banks[6]], writes=[nbn])
            op("pool", lambda e: e.memset(Qbd.ap(), 0.0), writes=[Qbd])
            qtp = bview(6, BF16, [4, 16], off=800)
            for pr in range(4):
                op("pe", lambda e, pr=pr: e.transpose(qtp[:, pr, :], s_qb[st].ap()[0:16, pr * 128:(pr + 1) * 128], identb.ap()[0:16, 0:16]),
                   reads=[s_qb[st], identb], writes=[banks[6]])
            op("dve", lambda e: e.tensor_copy(Qbd.ap()[0:64, :, 0:16], qtp[0:64, :, :]), reads=[banks[6]], writes=[Qbd])
            op("dve", lambda e: e.tensor_copy(Qbd.ap()[64:128, :, 16:32], qtp[64:128, :, :]), reads=[banks[6]], writes=[Qbd])
            oA = bview(5, F32, [4, 65], parts=16)
            oB = bview(7, F32, [4, 65], parts=16)
            blk = 0
            for g0 in range(0, NJ, G):
                par = gi % 2
                gi += 1
                if g0 + G < NJ:
                    load_cache(st, g0 + G, gi % 2)
                elif st + 1 < NSTREAM:
                    pass
                op("dve", lambda e, par=par: e.tensor_copy(kc_b[par].ap(), kc_f[par].ap()), reads=[kc_f[par]], writes=[kc_b[par]])
                op("pool", lambda e, par=par: e.tensor_copy(vc_b[par].ap()[:, :, :, 0:64], vc_f[par].ap().rearrange("p g (h d) -> p g h d", h=8)),
                   reads=[vc_f[par]], writes=[vc_b[par]])
                for jj in range(G):
                    j = g0 + jj
                    bp = blk % 2
                    blk += 1
                    ktp = bview(6, BF16, [4, 128])
                    for pr in range(4):
                        op("pe", lambda e, pr=pr, jj=jj, par=par, ktp=ktp: e.transpose(ktp[:, pr, :], kc_b[par].ap()[:, jj, pr * 128:(pr + 1) * 128], identb.ap()),
                           reads=[kc_b[par], identb], writes=[banks[6]])
                    op("dve", lambda e, bp=bp, ktp=ktp: e.tensor_copy(k2T[bp].ap(), ktp), reads=[banks[6]], writes=[k2T[bp]])
                    scs = bview(6, F32, [4, 32], off=256)
                    for pr in range(4):
                        op("pe", lambda e, pr=pr, bp=bp, scs=scs: e.matmul(scs[:, pr, :], k2T[bp].ap()[:, pr, :], Qbd.ap()[:, pr, :], start=True, stop=True),
                           reads=[k2T[bp], Qbd], writes=[banks[6]])
                    op("dve", lambda e, bp=bp, j=j, scs=scs: e.tensor_tensor(sbs[bp].ap(), scs.rearrange("p a (b t) -> p (a b) t", b=2),
                                                                             nb.ap()[:, j, :].unsqueeze(2).to_broadcast([128, 8, 16]), ALU.add),
                       reads=[banks[6], nb], writes=[sbs[bp]])
                    op("act", lambda e, bp=bp: e.activation(out=pTs[bp].ap(), in_=sbs[bp].ap(), func=AF.Exp), reads=[sbs[bp]], writes=[pTs[bp]])
                    for hh in range(8):
                        o_ = (oA if hh < 4 else oB)[:, hh % 4, :]
                        op("pe", lambda e, hh=hh, bp=bp, jj=jj, par=par, o_=o_, j=j: e.matmul(o_, pTs[bp].ap()[:, hh, :], vc_b[par].ap()[:, jj, hh, :], start=(j == 0 and hh % 4 == 0), stop=False, skip_group_check=True),
                           reads=[pTs[bp], vc_b[par]], writes=[banks[5 + 2 * (hh // 4)]])
            ktp = bview(6, BF16, [4, 16])
            for pr in range(4):
                op("pe", lambda e, pr=pr: e.transpose(ktp[:, pr, :], s_kb[st].ap()[0:16, pr * 128:(pr + 1) * 128], identb.ap()[0:16, 0:16]),
                   reads=[s_kb[st], identb], writes=[banks[6]])
            op("dve", lambda e: e.tensor_copy(k2T[0].ap()[:, :, 0:16], ktp), reads=[banks[6]], writes=[k2T[0]])
            scs = bview(6, F32, [4, 32], parts=16, off=256)
            for pr in range(4):
                op("pe", lambda e, pr=pr: e.matmul(scs[:, pr, :], k2T[0].ap()[:, pr, 0:16], Qbd.ap()[:, pr, :], start=True, stop=True), reads=[k2T[0], Qbd], writes=[banks[6]])
            op("dve", lambda e: e.tensor_tensor(sbs[0].ap()[0:16], scs.rearrange("p a (b t) -> p (a b) t", b=2), nbn.ap().unsqueeze(2).to_broadcast([16, 8, 16]), ALU.add),
               reads=[banks[6], nbn], writes=[sbs[0]])
            op("dve", lambda e: e.tensor_tensor(sbs[0].ap()[0:16], sbs[0].ap()[0:16], cmask.ap().unsqueeze(1).to_broadcast([16, 8, 16]), ALU.add),
               reads=[sbs[0], cmask], writes=[sbs[0]])
            op("act", lambda e: e.activation(out=pTs[0].ap()[0:16], in_=sbs[0].ap()[0:16], func=AF.Exp), reads=[sbs[0]], writes=[pTs[0]])
            for hh in range(8):
                o_ = (oA if hh < 4 else oB)[:, hh % 4, :]
                op("pe", lambda e, hh=hh, o_=o_: e.matmul(o_, pTs[0].ap()[0:16, hh, :], s_vb[st].ap()[0:16, hh, :], start=False, stop=True, skip_group_check=True),
                   reads=[pTs[0], s_vb[st]], writes=[banks[5 + 2 * (hh // 4)]])
            op("dve", lambda e: e.tensor_copy(osb_s.ap()[:, 0:4, :], oA), reads=[banks[5]], writes=[osb_s])
            op("dve", lambda e: e.tensor_copy(osb_s.ap()[:, 4:8, :], oB), reads=[banks[7]], writes=[osb_s])
            op("dve", lambda e: e.reciprocal(rec_s.ap(), osb_s.ap()[:, :, 64:65]), reads=[osb_s], writes=[rec_s])
            op("dve", lambda e: e.tensor_tensor(yf_s.ap().rearrange("p (h d) -> p h d", h=8), osb_s.ap()[:, :, 0:64], rec_s.ap().to_broadcast([16, 8, 64]), ALU.mult),
               reads=[osb_s, rec_s], writes=[yf_s])
            ytp = bview(6, BF16, [4, 16], off=864)
            for c in range(4):
                op("pe", lambda e, c=c: e.transpose(ytp[:, c, :], yf_s.ap()[:, c * 128:(c + 1) * 128], identb.ap()[0:16, 0:16]), reads=[yf_s, identb], writes=[banks[6]])
            op("dve", lambda e: e.tensor_copy(yfT_sb.ap(), ytp), reads=[banks[6]], writes=[yfT_sb])
            t0 = (NT + st) * 128
            P.dma("sp", yfT_s[:, t0:t0 + 16].rearrange("(c p) t -> p c t", p=128), yfT_sb.ap(), reads=[yfT_sb])
        P.pop()
        P.pop()

        if STOP == 4:
            raise _Stop()
        P.push()
        w_out_t = alloc_weight("w_out", 12, D)
        w_up_t = alloc_weight("w_up", 8, 4096)
        w_dn_t = alloc_weight("w_dn", 32, D)
        for tl, src in ((w_out_t, wo_s), (w_up_t, wu_s), (w_dn_t, wdn_s)):
            for kc, wt in enumerate(tl):
                P.dma("sp", wt.ap(), src[kc * 128:(kc + 1) * 128, :], writes=[wt])
        x3 = [P.sb("x3_%d" % i, [128, D], F32) for i in range(2)]
        yn3 = [P.sb("yn3_%d" % i, [128, D], BF16) for i in range(2)]
        yfT3 = [P.sb("yfT3_%d" % i, [128, 4, 128], BF16) for i in range(2)]
        ynT = P.sb("ynT", [128, 8, 128], BF16)
        junk3 = P.sb("junk3", [128, D], BF16)
        st3 = P.sb("st3", [128, 4], F32)
        h2 = P.sb("h2", [128, D], BF16)
        h2T = P.sb("h2T", [128, 8, 128], BF16)
        rl = [P.sb("rl%d" % i, [128, 4, 128], BF16) for i in range(2)]
        aT = P.sb("aT", [128, 32, 128], BF16)

        def load3(ti, par):
            if ti >= NTT:
                return
            if ti < NT:
                P.dma("sp", x3[par].ap(), x_prompt[ti * 128:(ti + 1) * 128, :], writes=[x3[par]])
            else:
                op("pool", lambda e: e.memset(x3[par].ap(), 0.0), writes=[x3[par]])
                P.dma("sp", x3[par].ap()[0:T, :], x_sample[ti - NT], writes=[x3[par]])
            P.dma("sp", yn3[par].ap(), yn_s[ti * 128:(ti + 1) * 128, :], writes=[yn3[par]])
            P.dma("sp", yfT3[par].ap(), yfT_s[:, ti * 128:(ti + 1) * 128].rearrange("(c p) t -> p c t", p=128), writes=[yfT3[par]])

        load3(0, 0)
        for ti in range(NTT):
            par = ti % 2
            X = x3[par]
            load3(ti + 1, 1 - par)
            tp = bview(0, BF16, [8, 128])
            for c in range(8):
                op("pe", lambda e, c=c: e.transpose(tp[:, c, :], yn3[par].ap()[:, c * 128:(c + 1) * 128], identb.ap()), reads=[yn3[par], identb], writes=[banks[0]])
            op("dve", lambda e: e.tensor_copy(ynT.ap(), tp), reads=[banks[0]], writes=[ynT])
            for cg in range(2):
                ob = bview(1 + cg, F32, [512])
                for kc in range(12):
                    lhs_t = ynT if kc < 8 else yfT3[par]
                    lhs = ynT.ap()[:, kc, :] if kc < 8 else yfT3[par].ap()[:, kc - 8, :]
                    op("pe", lambda e, kc=kc, cg=cg, ob=ob, lhs=lhs: e.matmul(ob, lhs, w_out_t[kc].ap()[:, cg * 512:(cg + 1) * 512], start=(kc == 0), stop=(kc == 11)),
                       reads=[lhs_t, w_out_t[kc]], writes=[banks[1 + cg]])
                op("dve", lambda e, cg=cg, ob=ob: e.tensor_tensor(X.ap()[:, cg * 512:(cg + 1) * 512], X.ap()[:, cg * 512:(cg + 1) * 512], ob, ALU.add),
                   reads=[X, banks[1 + cg]], writes=[X])
            op("act", lambda e: e.activation(out=junk3.ap(), in_=X.ap(), func=AF.Square, accum_out=st3.ap()[:, 0:1]), reads=[X], writes=[st3])
            op("act", lambda e: e.activation(out=st3.ap()[:, 1:2], in_=st3.ap()[:, 0:1], func=AF.Ln, scale=1.0 / D, bias=EPS), reads=[st3], writes=[st3])
            op("act", lambda e: e.activation(out=st3.ap()[:, 2:3], in_=st3.ap()[:, 1:2], func=AF.Exp, scale=-0.5), reads=[st3], writes=[st3])
            op("dve", lambda e: e.tensor_scalar_mul(h2.ap(), X.ap(), st3.ap()[:, 2:3]), reads=[X, st3], writes=[h2])
            for c in range(8):
                op("pe", lambda e, c=c: e.transpose(tp[:, c, :], h2.ap()[:, c * 128:(c + 1) * 128], identb.ap()), reads=[h2, identb], writes=[banks[0]])
            op("dve", lambda e: e.tensor_copy(h2T.ap(), tp), reads=[banks[0]], writes=[h2T])
            for f4 in range(8):
                ub = bview(3 + f4 % 2, F32, [4, 128])
                up_ = f4 % 2
                for j in range(4):
                    fc = f4 * 4 + j
                    for kc in range(8):
                        op("pe", lambda e, kc=kc, fc=fc, j=j, ub=ub: e.matmul(ub[:, j, :], w_up_t[kc].ap()[:, fc * 128:(fc + 1) * 128], h2T.ap()[:, kc, :], start=(kc == 0), stop=(kc == 7)),
                           reads=[w_up_t[kc], h2T], writes=[banks[3 + up_]])
                op("act", lambda e, ub=ub, up_=up_: e.activation(out=rl[up_].ap(), in_=ub, func=AF.Relu), reads=[banks[3 + up_]], writes=[rl[up_]])
                op("pool", lambda e, f4=f4, up_=up_: e.tensor_tensor(aT.ap()[:, f4 * 4:(f4 + 1) * 4, :], rl[up_].ap(), rl[up_].ap(), ALU.mult), reads=[rl[up_]], writes=[aT])
            for cg in range(2):
                db = bview(5 + cg, F32, [512])
                for fc in range(32):
                    op("pe", lambda e, fc=fc, cg=cg, db=db: e.matmul(db, aT.ap()[:, fc, :], w_dn_t[fc].ap()[:, cg * 512:(cg + 1) * 512], start=(fc == 0), stop=(fc == 31)),
                       reads=[aT, w_dn_t[fc]], writes=[banks[5 + cg]])
                op("dve", lambda e, cg=cg, db=db: e.tensor_tensor(X.ap()[:, cg * 512:(cg + 1) * 512], X.ap()[:, cg * 512:(cg + 1) * 512], db, ALU.add),
                   reads=[X, banks[5 + cg]], writes=[X])
            if ti < NT:
                P.dma("pool", y_prompt[ti * 128:(ti + 1) * 128, :], X.ap(), reads=[X])
            else:
                P.dma("pool", y_sample[ti - NT], X.ap()[0:T, :], reads=[X])
        P.pop()
        P.finish()
      except _Stop:
        P.finish()
    return nc, P


_CACHE = {}


def kernel(x_prompt, x_sample, cache_k, cache_v, cache_logf, state_ssm, state_conv,
           norm1_w, w_in, conv_w, conv_b, dt_bias, A_log, D_skip, ssd_norm_w, f_bias,
           q_norm_w, k_norm_w, w_out, norm2_w, w_up, w_down, n_cores=8):
    f = lambda a: np.ascontiguousarray(np.asarray(a, dtype=np.float32))
    x_prompt = f(x_prompt)
    B, S, _ = x_prompt.shape
    x_sample = f(x_sample)
    DB, T = x_sample.shape[0], x_sample.shape[1]
    PAST = cache_k.shape[2]
    assert B == n_cores and DB % n_cores == 0
    NS = DB // n_cores
    key = (S, PAST, NS, T)
    if key not in _CACHE:
        _CACHE[key] = build_program(S, PAST, NS, T)[0]
    nc = _CACHE[key]
    ck = f(cache_k)[0].reshape(DB, PAST, 512)
    cv = f(cache_v)[0].reshape(DB, PAST, 512)
    clf = f(cache_logf)[0]
    ssm = f(state_ssm)[0].reshape(DB, 1024, 128)
    scv = f(state_conv)[0]
    shared = {"norm1_w": f(norm1_w)[0], "w_in": f(w_in)[0], "conv_w": f(conv_w)[0], "conv_b": f(conv_b)[0],
              "dt_bias": f(dt_bias)[0], "A_log": f(A_log)[0], "D_skip": f(D_skip)[0], "ssd_norm_w": f(ssd_norm_w)[0],
              "f_bias": f(f_bias)[0], "q_norm_w": f(q_norm_w)[0], "k_norm_w": f(k_norm_w)[0], "w_out": f(w_out)[0],
              "norm2_w": f(norm2_w)[0], "w_up": f(w_up)[0], "w_down": f(w_down)[0]}
    in_maps = []
    for c in range(n_cores):
        sl = slice(c * NS, (c + 1) * NS)
        m = {"x_prompt": x_prompt[c], "x_sample": x_sample[sl], "cache_k": ck[sl], "cache_v": cv[sl],
             "cache_logf": clf[sl], "state_ssm": ssm[sl], "state_conv": scv[sl]}
        m.update(shared)
        in_maps.append(m)
    res = run_bass_kernel_spmd(nc, in_maps, core_ids=list(range(n_cores))).results
    global _LAST
    _LAST = res
    cat = lambda n: np.stack([np.asarray(r[n]) for r in res])
    y_prompt = cat("y_prompt")
    y_sample = cat("y_sample").reshape(DB, T, D)
    p_k = cat("p_k").reshape(1, B, S, 8, 64)
    p_v = cat("p_v").reshape(1, B, S, 8, 64)
    p_logf = cat("p_logf").reshape(1, B, S, 8)
    p_ssm = cat("p_ssm").reshape(1, B, 16, 64, 128)
    p_conv = cat("p_conv").reshape(1, B, 3, 1536)
    s_k = cat("s_k").reshape(1, DB, T, 8, 64)
    s_v = cat("s_v").reshape(1, DB, T, 8, 64)
    s_logf = cat("s_logf").reshape(1, DB, T, 8)
    s_ssm = cat("s_ssm").reshape(1, DB, 16, 64, 128)
    s_conv = cat("s_conv").reshape(1, DB, 3, 1536)
    return (y_prompt, y_sample, p_k, p_v, p_logf, p_ssm, p_conv, s_k, s_v, s_logf, s_ssm, s_conv)
```

```python
import numpy as np
import concourse.bass as bass
import concourse.mybir as mybir
from concourse.bass_utils import run_bass_kernel_spmd

F32 = mybir.dt.float32
BF16 = mybir.dt.bfloat16
U8 = mybir.dt.uint8
AF = mybir.ActivationFunctionType
ALU = mybir.AluOpType
AX = mybir.AxisListType

_DT_SIZE = {F32: 4, BF16: 2, U8: 1}


class Buf:
    __slots__ = ("lw", "rd")

    def __init__(self):
        self.lw = None
        self.rd = {}


class Tile:
    def __init__(self, ap, nbufs=None):
        self._ap = ap
        self.buf = Buf()

    def ap(self):
        return self._ap


class _Rec:
    def __init__(self):
        self.call = None

    def __getattr__(self, name):
        def f(*a, **k):
            self.call = (name, a, k)
            return self
        return f


class Op:
    __slots__ = ("eng", "call", "inc", "reads", "writes", "busy", "lat", "preds", "nsucc", "succs", "seq", "idx", "dma_q")

    def __init__(self, eng, call, reads, writes, busy, lat, dma_q=None):
        self.eng = eng
        self.call = call
        self.reads = reads
        self.writes = writes
        self.busy = busy
        self.lat = lat
        self.dma_q = dma_q
        self.preds = ()
        self.succs = []
        self.seq = None


def _free_elems(ap):
    sh = ap.shape
    n = 1
    for d in sh[1:]:
        n *= int(d)
    return n


LAT_EXTRA = 60.0


class Prog:
    ENG = ("pe", "act", "dve", "pool", "sp")
    N_DMA_SEMS = {"sp": 28, "act": 4, "pool": 8}

    def __init__(self, nc, sbuf_bytes=None):
        self.nc = nc
        self.segments = [[]]
        self.sb_off = 0
        self.sb_stack = []
        self.ps_off = 0
        self.ps_stack = []
        self.sbuf_bytes = sbuf_bytes or 196608
        self.n_instr = 0

    def __enter__(self):
        from contextlib import ExitStack
        self.es = ExitStack()
        nc = self.nc
        self.sb_arena = self.es.enter_context(nc.sbuf_tensor("arena", [128, self.sbuf_bytes], U8))
        self.ps_arena = self.es.enter_context(nc.psum_tensor("psarena", [128, 16384], U8))
        self.sems = {}
        for e in ("pe", "act", "dve", "pool"):
            self.sems[e] = self.es.enter_context(nc.semaphore("s_" + e))
        for q, n in self.N_DMA_SEMS.items():
            for i in range(n):
                k = ("dma", q, i)
                self.sems[k] = self.es.enter_context(nc.semaphore("d_%s%d" % (q, i)))
        return self

    def __exit__(self, *a):
        return self.es.__exit__(*a)

    def sb(self, name, shape, dtype, parts=None):
        n = int(np.prod(shape[1:])) * _DT_SIZE[dtype]
        n_al = (n + 31) // 32 * 32
        off = self.sb_off
        assert off + n_al <= self.sbuf_bytes, "SBUF arena overflow at %s (%d)" % (name, off + n_al)
        self.sb_off += n_al
        self.sb_hwm = max(getattr(self, "sb_hwm", 0), self.sb_off)
        ap = self.sb_arena[0:shape[0], off:off + n].bitcast(dtype)
        ap = self._reshape(ap, shape)
        t_ = Tile(ap)
        t_.name = name
        return t_

    def ps(self, name, shape, dtype):
        n = int(np.prod(shape[1:])) * _DT_SIZE[dtype]
        n_al = (n + 2047) // 2048 * 2048
        off = self.ps_off
        assert off + n_al <= 16384, "PSUM overflow at %s" % name
        self.ps_off += n_al
        ap = self.ps_arena[0:shape[0], off:off + n].bitcast(dtype)
        ap = self._reshape(ap, shape)
        t = Tile(ap)
        t.name = name
        t.buf.rd = "psum"
        return t

    @staticmethod
    def _reshape(ap, shape):
        if len(shape) == 2:
            return ap
        names = "abcdefg"[:len(shape) - 1]
        kw = {names[i]: shape[i + 1] for i in range(len(shape) - 1)}
        return ap.rearrange("p (%s) -> p %s" % (" ".join(names), " ".join(names)), **kw)

    def push(self):
        self.sb_stack.append(self.sb_off)
        self.ps_stack.append(self.ps_off)

    def pop(self):
        self.barrier()
        self.sb_off = self.sb_stack.pop()
        self.ps_off = self.ps_stack.pop()

    def op(self, eng, fn, reads=(), writes=()):
        rec = _Rec()
        fn(rec)
        name, a, k = rec.call
        if eng == "pe":
            if name == "matmul":
                rhs = a[2] if len(a) > 2 else k["rhs"]
                n = _free_elems(rhs)
                mult = 4.0 if rhs.dtype == F32 else 1.0
                busy = mult * max(n, 64) / 2.0 + 16
            else:
                busy = 80.0
        else:
            out = a[0] if a else k.get("out", k.get("ap"))
            n = _free_elems(out)
            if name == "reciprocal":
                busy = (70 + 8 * n) / 0.96
            elif eng == "act":
                busy = (224 + n) / 1.2
            elif eng == "dve":
                busy = 380 + n / 0.96
            else:
                busy = 300 + n * 2.1
        o = Op(eng, rec.call, tuple(reads), tuple(writes), busy, busy + LAT_EXTRA)
        self.segments[-1].append(o)
        self.n_instr += 1

    def dma(self, q, out, in_, reads=(), writes=(), **kw):
        nbytes = _free_elems(out) * out.shape[0] * _DT_SIZE.get(out.dtype, 4)
        busy = 70.0 if q != "pool" else 900.0
        lat = 2200.0 + nbytes / 120.0
        o = Op(q, ("dma_start", (), dict(out=out, in_=in_, **kw)), tuple(reads), tuple(writes), busy, lat, dma_q=q)
        self.segments[-1].append(o)
        self.n_instr += 1

    def barrier(self):
        if self.segments[-1]:
            self.segments.append([])

    def make_identity(self, tile, dtype, n=128):
        ap = tile.ap()
        self.op("pool", lambda e: e.memset(ap, 1.0), writes=[tile])
        self.op("pool", lambda e: e.affine_select(out=ap, in_=ap, pattern=[[-1, n]], compare_op=ALU.is_equal,
                                                  fill=0.0, base=0, channel_multiplier=1), reads=[tile], writes=[tile])

    def _schedule_segment(self, ops, t0):
        import heapq
        lastw = {}
        readers = {}
        for i, o in enumerate(ops):
            o.idx = i
            preds = set()
            for t in o.reads:
                b = id(t.buf)
                w = lastw.get(b)
                if w is not None:
                    preds.add(w)
                if t.buf.rd == "psum":
                    r = readers.get(b)
                    if r:
                        preds.add(r[-1])
            for t in o.writes:
                b = id(t.buf)
                w = lastw.get(b)
                if w is not None:
                    preds.add(w)
                r = readers.get(b)
                if r:
                    preds.update(r)
            preds.discard(i)
            o.preds = tuple(preds)
            o.nsucc = 0
            o.succs = []
            for t in o.reads:
                readers.setdefault(id(t.buf), []).append(i)
            for t in o.writes:
                b = id(t.buf)
                lastw[b] = i
                readers[b] = []
        pending = [len(o.preds) for o in ops]
        for o in ops:
            for p in o.preds:
                ops[p].succs.append(o.idx)
        n = len(ops)
        blevel = [0.0] * n
        for i in range(n - 1, -1, -1):
            o = ops[i]
            m = 0.0
            for s_ in o.succs:
                if blevel[s_] > m:
                    m = blevel[s_]
            blevel[i] = o.lat + m
        avail = [t0] * n
        finish = [0.0] * n
        pend = {e: [] for e in self.ENG}
        rdy = {e: [] for e in self.ENG}
        efree = {e: t0 for e in self.ENG}
        for o in ops:
            if pending[o.idx] == 0:
                heapq.heappush(pend[o.eng], (t0, o.idx))
        order = []
        while len(order) < n:
            best = None
            for e in self.ENG:
                pe_, re_ = pend[e], rdy[e]
                while pe_ and pe_[0][0] <= efree[e]:
                    a_, i_ = heapq.heappop(pe_)
                    heapq.heappush(re_, (-blevel[i_], i_))
                if re_:
                    st = efree[e]
                elif pe_:
                    st = pe_[0][0]
                else:
                    continue
                if best is None or st < best[0]:
                    best = (st, e)
            assert best is not None, "cycle in DAG?"
            st, e = best
            if not rdy[e]:
                pe_ = pend[e]
                while pe_ and pe_[0][0] <= st:
                    a_, i_ = heapq.heappop(pe_)
                    heapq.heappush(rdy[e], (-blevel[i_], i_))
            _, i = heapq.heappop(rdy[e])
            o = ops[i]
            efree[e] = st + o.busy
            finish[i] = st + o.lat
            fin_same = st + o.busy + (0.0 if e == "pe" else 120.0)
            order.append(i)
            for s_ in o.succs:
                pending[s_] -= 1
                f_ = finish[i]
                if avail[s_] < f_:
                    avail[s_] = f_
                if pending[s_] == 0:
                    heapq.heappush(pend[ops[s_].eng], (avail[s_], s_))
        assert len(order) == len(ops), "cycle in DAG?"
        tend = max([t0] + [finish[i] for i in order])
        return order, tend

    def finish(self):
        nc = self.nc
        sems = self.sems
        count = {e: 0 for e in ("pe", "act", "dve", "pool")}
        dma_rr = {q: 0 for q in self.N_DMA_SEMS}
        dma_val = {k: 0 for k in sems if isinstance(k, tuple)}
        seen = {e: {} for e in self.ENG}
        queues = {e: [] for e in self.ENG}
        t = 0.0

        def full_barrier():
            cur = [(e, count[e]) for e in count if count[e] > 0]
            cur += [(k, v) for k, v in dma_val.items() if v > 0]
            for e in self.ENG:
                waits = []
                for k, v in cur:
                    if k == e and e == "pe":
                        continue
                    if seen[e].get(k, 0) < v:
                        seen[e][k] = v
                        waits.append((k, v))
                if waits:
                    queues[e].append((waits, None, None))

        for seg in self.segments:
            if not seg:
                continue
            order, t = self._schedule_segment(seg, t)
            for i in order:
                o = seg[i]
                e = o.eng
                waits = []
                sn = seen[e]
                for p in o.preds:
                    k, v = seg[p].seq
                    if k == "pe" and e == "pe":
                        continue
                    if sn.get(k, 0) >= v:
                        continue
                    sn[k] = v
                    waits.append((k, v))
                if o.dma_q is not None:
                    q = o.dma_q
                    j = dma_rr[q]
                    dma_rr[q] = (j + 1) % self.N_DMA_SEMS[q]
                    key = ("dma", q, j)
                    prev = dma_val[key]
                    if prev > 0 and sn.get(key, 0) < prev:
                        sn[key] = prev
                        waits.append((key, prev))
                    dma_val[key] = prev + 16
                    o.seq = (key, prev + 16)
                    queues[e].append((waits, o.call, (key, 16)))
                else:
                    count[e] += 1
                    o.seq = (e, count[e])
                    queues[e].append((waits, o.call, (e, 1)))
            full_barrier()
        self.est_ns = t

        def replay(name, eng):
            for waits, call, inc in queues[name]:
                for k, v in waits:
                    eng.wait_ge(sems[k], v)
                if call is not None:
                    name_, a_, k_ = call
                    ins = getattr(eng, name_)(*a_, **k_)
                    ins.then_inc(sems[inc[0]], inc[1])

        with nc.Block() as block:
            @block.tensor
            def _(e):
                replay("pe", e)

            @block.scalar
            def _(e):
                replay("act", e)

            @block.vector
            def _(e):
                replay("dve", e)

            @block.gpsimd
            def _(e):
                replay("pool", e)

            @block.sync
            def _(e):
                replay("sp", e)


D = 1024
DIN = 4120
C_Z, C_XBC, C_DT, C_Q, C_K, C_V, C_F = 0, 1024, 2560, 2576, 3088, 3600, 4112
EPS = 1e-6
NEG = -30000.0


class _Stop(Exception):
    pass


def build_program(S, PAST, NSTREAM=4, T=16):
    import os
    STOP = int(os.environ.get("K_STOP", "99"))
    NT1 = int(os.environ.get("K_NT1", "999"))
    SUB = int(os.environ.get("K_SUB", "99"))

    def chk(n):
        if SUB == n:
            raise _Stop()
    NT = S // 128
    NQG = S // 512
    NTT = NT + NSTREAM
    NJ = PAST // 128
    nc = bass.Bass("TRN2", target_bir_lowering=False)

    def din(name, shape):
        return nc.dram_tensor(name, list(shape), F32, kind="ExternalInput").ap()

    def dout(name, shape):
        return nc.dram_tensor(name, list(shape), F32, kind="ExternalOutput").ap()

    x_prompt = din("x_prompt", [S, D])
    x_sample = din("x_sample", [NSTREAM, T, D])
    cache_k = din("cache_k", [NSTREAM, PAST, 512])
    cache_v = din("cache_v", [NSTREAM, PAST, 512])
    cache_logf = din("cache_logf", [NSTREAM, PAST, 8])
    state_ssm = din("state_ssm", [NSTREAM, 1024, 128])
    state_conv = din("state_conv", [NSTREAM, 3, 1536])
    norm1_w = din("norm1_w", [D])
    w_in = din("w_in", [D, DIN])
    conv_w = din("conv_w", [4, 1536])
    conv_b = din("conv_b", [1536])
    dt_bias = din("dt_bias", [16])
    A_log = din("A_log", [16])
    D_skip = din("D_skip", [16])
    ssd_norm_w = din("ssd_norm_w", [D])
    f_bias = din("f_bias", [8])
    q_norm_w = din("q_norm_w", [64])
    k_norm_w = din("k_norm_w", [64])
    w_out = din("w_out", [1536, D])
    norm2_w = din("norm2_w", [D])
    w_up = din("w_up", [D, 4096])
    w_down = din("w_down", [4096, D])

    y_prompt = dout("y_prompt", [S, D])
    y_sample = dout("y_sample", [NSTREAM, T, D])
    p_k = dout("p_k", [S, 512])
    p_v = dout("p_v", [S, 512])
    p_logf = dout("p_logf", [S, 8])
    p_ssm = dout("p_ssm", [1024, 128])
    p_conv = dout("p_conv", [3, 1536])
    s_k = dout("s_k", [NSTREAM, T, 512])
    s_v = dout("s_v", [NSTREAM, T, 512])
    s_logf = dout("s_logf", [NSTREAM, T, 8])
    s_ssm = dout("s_ssm", [NSTREAM, 1024, 128])
    s_conv = dout("s_conv", [NSTREAM, 3, 1536])

    SK = "ExternalOutput" if os.environ.get("K_DBG") else "Internal"
    qT_s = nc.dram_tensor("qT_s", [8, 70, S], BF16, kind=SK).ap()
    kT_s = nc.dram_tensor("kT_s", [8, 70, S], BF16, kind=SK).ap()
    yn_s = nc.dram_tensor("yn_s", [NTT * 128, D], BF16, kind=SK).ap()
    yfT_s = nc.dram_tensor("yfT_s", [512, NTT * 128], BF16, kind=SK).ap()
    vS = nc.dram_tensor("vS", [8, NT, 128, 128], BF16, kind=SK).ap()
    wo_s = nc.dram_tensor("wo_s", [1536, D], BF16, kind="Internal").ap()
    wu_s = nc.dram_tensor("wu_s", [D, 4096], BF16, kind="Internal").ap()
    wdn_s = nc.dram_tensor("wdn_s", [4096, D], BF16, kind="Internal").ap()

    P = Prog(nc, sbuf_bytes=210944)
    with P:
      try:
        banks = [P.ps("bank%d" % i, [128, 512], F32) for i in range(8)]

        def bview(i, dtype, shape, parts=128, off=0):
            n = int(np.prod(shape))
            ap = banks[i].ap()[0:parts, :]
            if dtype != F32:
                ap = ap.bitcast(dtype)
            ap = ap[:, off:off + n]
            return Prog._reshape(ap, [parts] + list(shape))

        def op(eng, fn, reads=(), writes=()):
            P.op(eng, fn, reads, writes)

        identb = P.sb("identb", [128, 128], BF16)
        identf = P.sb("identf", [128, 128], F32)
        ones_f = P.sb("ones_f", [128, 128], F32)
        ones_b = P.sb("ones_b", [128, 512], BF16)
        LE_f = P.sb("LE_f", [128, 128], F32)
        GT_f = P.sb("GT_f", [128, 128], F32)
        GT_b = P.sb("GT_b", [128, 128], BF16)
        P.make_identity(identb, BF16)
        P.make_identity(identf, F32)
        op("pool", lambda e: e.memset(ones_f.ap(), 1.0), writes=[ones_f])
        op("pool", lambda e: e.memset(ones_b.ap(), 1.0), writes=[ones_b])
        op("pool", lambda e: e.memset(LE_f.ap(), 1.0), writes=[LE_f])
        op("pool", lambda e: e.affine_select(out=LE_f.ap(), in_=LE_f.ap(), pattern=[[1, 128]], compare_op=ALU.is_ge,
                                             fill=0.0, base=0, channel_multiplier=-1), reads=[LE_f], writes=[LE_f])
        op("pool", lambda e: e.memset(GT_f.ap(), 1.0), writes=[GT_f])
        op("pool", lambda e: e.affine_select(out=GT_f.ap(), in_=GT_f.ap(), pattern=[[-1, 128]], compare_op=ALU.is_gt,
                                             fill=0.0, base=0, channel_multiplier=1), reads=[GT_f], writes=[GT_f])

        op("pool", lambda e: e.tensor_copy(GT_b.ap(), GT_f.ap()), reads=[GT_f], writes=[GT_b])

        def bc_load(name, src, n):
            t = P.sb(name, [128, n], F32)
            P.dma("sp", t.ap(), src.partition_broadcast(128), writes=[t])
            return t

        dtb_b = bc_load("dtb_b", dt_bias, 16)
        negA_b = bc_load("negA_b", A_log, 16)
        D_b = bc_load("D_b", D_skip, 16)
        fb_b = bc_load("fb_b", f_bias, 8)
        qw_b = bc_load("qw_b", q_norm_w, 64)
        kw_b = bc_load("kw_b", k_norm_w, 64)
        op("act", lambda e: e.activation(out=negA_b.ap(), in_=negA_b.ap(), func=AF.Exp), reads=[negA_b], writes=[negA_b])
        op("dve", lambda e: e.tensor_scalar_mul(negA_b.ap(), negA_b.ap(), -1.0), reads=[negA_b], writes=[negA_b])
        op("dve", lambda e: e.tensor_scalar_mul(qw_b.ap(), qw_b.ap(), 0.125), reads=[qw_b], writes=[qw_b])
        negfb = P.sb("negfb", [8, 1], F32)
        P.dma("sp", negfb.ap(), f_bias.rearrange("(h o) -> h o", o=1), writes=[negfb])
        op("dve", lambda e: e.tensor_scalar_mul(negfb.ap(), negfb.ap(), -1.0), reads=[negfb], writes=[negfb])
        cw = P.sb("cw", [128, 12, 4], F32)
        cb = P.sb("cb", [128, 12], F32)
        for k in range(4):
            P.dma("sp", cw.ap()[:, :, k], conv_w[k].rearrange("(c p) -> p c", p=128), writes=[cw], allow_slow_non_contiguous=True)
        P.dma("sp", cb.ap(), conv_b.rearrange("(c p) -> p c", p=128), writes=[cb], allow_slow_non_contiguous=True)
        g1 = P.sb("g1", [128, 8], F32)
        g2 = P.sb("g2", [128, 8], F32)
        gn = P.sb("gn", [128, 8], F32)
        P.dma("sp", g1.ap(), norm1_w.rearrange("(c p) -> p c", p=128), writes=[g1], allow_slow_non_contiguous=True)
        P.dma("sp", g2.ap(), norm2_w.rearrange("(c p) -> p c", p=128), writes=[g2], allow_slow_non_contiguous=True)
        P.dma("sp", gn.ap(), ssd_norm_w.rearrange("(c p) -> p c", p=128), writes=[gn], allow_slow_non_contiguous=True)

        def alloc_weight(name, nk, ncols):
            return [P.sb("%s%d" % (name, kc), [128, ncols], BF16) for kc in range(nk)]

        def load_weight(tiles, src, nk, ncols, scale_t, scale_nk, stage):
            step = int(stage[0].ap().shape[1])
            i = 0
            for kc in range(nk):
                wt = tiles[kc]
                for c0 in range(0, ncols, step):
                    c1 = min(ncols, c0 + step)
                    st = stage[i % len(stage)]
                    i += 1
                    P.dma("sp", st.ap()[:, 0:c1 - c0], src[kc * 128:(kc + 1) * 128, c0:c1], writes=[st])
                    eng = ("act", "dve", "act", "pool", "act", "dve")[i % 6]
                    if scale_t is not None and kc < scale_nk and eng == "act":
                        op("act", lambda e, wt=wt, st=st, c0=c0, c1=c1, kc=kc: e.activation(
                            out=wt.ap()[:, c0:c1], in_=st.ap()[:, 0:c1 - c0], func=AF.Copy, scale=scale_t.ap()[:, kc:kc + 1]), reads=[st, scale_t], writes=[wt])
                    elif scale_t is not None and kc < scale_nk:
                        op(eng, lambda e, wt=wt, st=st, c0=c0, c1=c1, kc=kc: e.tensor_scalar_mul(
                            wt.ap()[:, c0:c1], st.ap()[:, 0:c1 - c0], scale_t.ap()[:, kc:kc + 1]), reads=[st, scale_t], writes=[wt])
                    else:
                        op(eng, lambda e, wt=wt, st=st, c0=c0, c1=c1: e.tensor_copy(
                            wt.ap()[:, c0:c1], st.ap()[:, 0:c1 - c0]), reads=[st], writes=[wt])

        seqs = []

        if STOP == 0:
            raise _Stop()
        P.push()
        vb_t = P.sb("vb_t", [128, 8, 128], BF16)
        op("pool", lambda e: e.memset(vb_t.ap(), 1.0), writes=[vb_t])
        s_qb = [P.sb("s_qb%d" % i, [128, 512], BF16) for i in range(NSTREAM)]
        s_kb = [P.sb("s_kb%d" % i, [128, 512], BF16) for i in range(NSTREAM)]
        s_vb = [P.sb("s_vb%d" % i, [128, 8, 65], BF16) for i in range(NSTREAM)]
        s_lf = [P.sb("s_lf%d" % i, [128, 8], F32) for i in range(NSTREAM)]
        for i in range(NSTREAM):
            op("pool", lambda e, i=i: e.memset(s_vb[i].ap(), 1.0), writes=[s_vb[i]])

        P.push()
        w_in_t = alloc_weight("w_in", 8, DIN)
        wd = P.sb("wd", [128, 12, 5, 128], BF16)
        for cc in range(12):
            for k in range(5):
                sc_ap = cw.ap()[:, cc, k:k + 1] if k < 4 else cb.ap()[:, cc:cc + 1]
                op("act", lambda e: e.activation(out=wd.ap()[:, cc, k, :], in_=identb.ap(), func=AF.Copy, scale=sc_ap), reads=[identb, cw, cb], writes=[wd])
        Dd = P.sb("Dd", [128, 16, 128], BF16)
        for h in range(16):
            op("act", lambda e: e.activation(out=Dd.ap()[:, h, :], in_=identb.ap(), func=AF.Copy, scale=D_b.ap()[:, h:h + 1]), reads=[identb, D_b], writes=[Dd])
        cst = P.sb("cst", [128, 12, 3], F32)
        cso = P.sb("cso", [128, 12, 3], F32)
        st_out = P.sb("st_out", [128, 8, 128], F32)
        for h in range(8):
            for c0 in range(0, S, 512):
                P.dma("sp", qT_s[h, 67:70, c0:c0 + 512], ones_b.ap()[0:3, :], reads=[ones_b])
                P.dma("sp", kT_s[h, 64:67, c0:c0 + 512], ones_b.ap()[0:3, :], reads=[ones_b])

        P.push()
        stage = [P.sb("stage%d" % i, [128, 2048], F32) for i in range(4)]
        load_weight(w_in_t, w_in, 8, DIN, g1, 8, stage)
        P.pop()

        ones8 = P.sb("ones8", [8, 128], F32)
        op("pool", lambda e: e.memset(ones8.ap(), 1.0), writes=[ones8])
        xt = [P.sb("xt%d" % i, [128, D], F32) for i in range(2)]
        junk = P.sb("junk", [128, D], BF16)
        st1 = P.sb("st1", [128, 8], F32)
        hb_2 = [P.sb("hb%d" % i, [128, D], BF16) for i in range(2)]
        hT_2 = [P.sb("hT%d" % i, [128, 8, 128], BF16) for i in range(2)]
        zs_2 = [P.sb("zs%d" % i_, [128, D], BF16) for i_ in range(2)]
        vtok = P.sb("vtok", [128, 512], F32)
        ssq = P.sb("ssq", [128, 8], F32)
        rs = P.sb("rs", [128, 8], F32)
        kn = P.sb("kn", [128, 512], F32)
        kn2 = P.sb("kn2", [128, 512], F32)
        sq = kn2
        qb = P.sb("qb", [128, 512], BF16)
        kb = P.sb("kb", [128, 512], BF16)
        qkT = P.sb("qkT", [64, 1, 8, 128], BF16)
        lft = P.sb("lft", [128, 8], F32)
        lfa = P.sb("lfa", [128, 8], F32)
        fT1 = P.sb("fT1", [8, 128], F32)
        fT2 = P.sb("fT2", [8, 128], F32)
        cT = P.sb("cT", [8, 128], F32)
        ccar = P.sb("ccar", [8, 1], F32)
        csp = P.sb("csp", [8, 6, 128], BF16)
        cr = P.sb("cr", [8, 128], F32)
        dtt = P.sb("dtt", [128, 16], F32)
        dt_ = P.sb("dt_", [128, 16], F32)
        dA = P.sb("dA", [128, 16], F32)
        xpre = [P.sb("xpre%d" % i, [128, 12, 131], BF16) for i in range(2)]
        xc_2 = [P.sb("xc%d" % i_, [128, 12, 128], BF16) for i_ in range(2)]
        xdt_2 = [P.sb("xdt%d" % i_, [128, D], BF16) for i_ in range(2)]
        xtok_2 = [P.sb("xtok%d" % i_, [128, D], BF16) for i_ in range(2)]
        xdtt_2 = [P.sb("xdtt%d" % i_, [128, D], BF16) for i_ in range(2)]
        Btok_2 = [P.sb("Btok%d" % i_, [128, 2, 128], BF16) for i_ in range(2)]
        cbm = P.sb("cbm", [128, 2, 128], F32)
        rseg_h = P.sb("rseg_h", [128, 16, 128], BF16)
        rseg_l = P.sb("rseg_l", [128, 16, 128], BF16)
        dAh = P.sb("dAh", [128, 16], BF16)
        dAl = P.sb("dAl", [128, 16], F32)
        decT = P.sb("decT", [128, 8, 128], BF16)
        MT_2 = [P.sb("MT%d" % i_, [128, 16, 128], BF16) for i_ in range(2)]
        eacs_2 = [P.sb("eacs%d" % i_, [128, 16], F32) for i_ in range(2)]
        tail = P.sb("tail", [128, 16], F32)
        cd_b_2 = [P.sb("cd_b%d" % i_, [128, 16], F32) for i_ in range(2)]
        dtail = P.sb("dtail", [128, 16], F32)
        t2 = P.sb("t2", [128, D], F32)
        gsb = P.sb("gsb", [128, D], F32)
        ssg = P.sb("ssg", [128, 4], F32)
        yn = P.sb("yn", [128, D], BF16)
        st_tmp = gsb

        class _View:
            def __init__(self, base, ap):
                self.buf = base.buf
                self._ap = ap

            def ap(self):
                return self._ap
        stT = P.sb("stT", [128, D], F32)
        stT_b = P.sb("stT_b", [128, D], BF16)

        def rstd_from(ssq_ap, out_ap, tmp_ap, n, tiles):
            op("act", lambda e: e.activation(out=tmp_ap, in_=ssq_ap, func=AF.Ln, scale=1.0 / n, bias=EPS), reads=tiles, writes=tiles)
            op("act", lambda e: e.activation(out=out_ap, in_=tmp_ap, func=AF.Exp, scale=-0.5), reads=tiles, writes=tiles)

        def load_x(ti, par):
            if ti < NT:
                P.dma("sp", xt[par].ap(), x_prompt[ti * 128:(ti + 1) * 128, :], writes=[xt[par]])
            elif ti < NTT:
                op("pool", lambda e: e.memset(xt[par].ap(), 0.0), writes=[xt[par]])
                P.dma("sp", xt[par].ap()[0:T, :], x_sample[ti - NT], writes=[xt[par]])

        def p1_vars(ti):
            par = ti % 2
            return (par, zs_2[par], xc_2[par], xdt_2[par], xtok_2[par], xdtt_2[par], Btok_2[par], MT_2[par], eacs_2[par], cd_b_2[par])

        def p1_A(ti):
            par, zs, xc, xdt, xtok, xdtt, Btok, MT, eacs, cd_b = p1_vars(ti)
            is_p = ti < NT
            L = 128 if is_p else T
            si = ti - NT
            X = xt[par]
            xp = xpre[par]
            xpo = xpre[1 - par]
            hb = hb_2[par]
            hT = hT_2[par]
            op("act", lambda e: e.activation(out=junk.ap(), in_=X.ap(), func=AF.Square, accum_out=st1.ap()[:, 0:1]), reads=[X], writes=[st1])
            rstd_from(st1.ap()[:, 0:1], st1.ap()[:, 2:3], st1.ap()[:, 1:2], D, [st1])
            op("act", lambda e: e.activation(out=hb.ap(), in_=X.ap(), func=AF.Copy, scale=st1.ap()[:, 2:3]), reads=[X, st1], writes=[hb])
            load_x(ti + 1, 1 - par)
            pT = bview(0, BF16, [8, 128])
            for c in range(8):
                op("pe", lambda e, c=c: e.transpose(pT[:, c, :], hb.ap()[:, c * 128:(c + 1) * 128], identb.ap()), reads=[hb, identb], writes=[banks[0]])
            op("dve", lambda e: e.tensor_copy(hT.ap(), pT), reads=[banks[0]], writes=[hT])

            def proj_tok(bank_i, c0, n, off=0):
                o = bview(bank_i, F32, [n], off=off)
                for kc in range(8):
                    op("pe", lambda e, kc=kc: e.matmul(o, hT.ap()[:, kc, :], w_in_t[kc].ap()[:, c0:c0 + n], start=(kc == 0), stop=(kc == 7)),
                       reads=[hT, w_in_t[kc]], writes=[banks[bank_i]])
                return o
            chk(1)
            def qk_norm(which, pp, bank_i):
                op("act", lambda e: e.activation(out=sq.ap(), in_=pp, func=AF.Square), reads=[banks[bank_i]], writes=[sq])
                op("dve", lambda e: e.tensor_reduce(out=ssq.ap(), in_=sq.ap().rearrange("p (h d) -> p h d", h=8), axis=AX.X, op=ALU.add), reads=[sq], writes=[ssq])
                rstd_from(ssq.ap(), rs.ap(), ssq.ap(), 64, [ssq, rs])
                op("dve", lambda e: e.tensor_tensor(kn.ap().rearrange("p (h d) -> p h d", h=8), pp.rearrange("p (h d) -> p h d", h=8),
                                                    rs.ap().unsqueeze(2).to_broadcast([128, 8, 64]), ALU.mult), reads=[banks[bank_i], rs], writes=[kn])
                if which == "k":
                    op("pool", lambda e: e.tensor_tensor(kn2.ap().rearrange("p (h d) -> p h d", h=8), kn.ap().rearrange("p (h d) -> p h d", h=8),
                                                         kw_b.ap().unsqueeze(1).to_broadcast([128, 8, 64]), ALU.mult), reads=[kn, kw_b], writes=[kn2])
                    kdst = kb if is_p else s_kb[si]
                    op("pool", lambda e: e.tensor_copy(kdst.ap(), kn2.ap()), reads=[kn2], writes=[kdst])
                    if is_p:
                        P.dma("sp", p_k[ti * 128:(ti + 1) * 128, :], kn2.ap(), reads=[kn2])
                    else:
                        P.dma("sp", s_k[si], kn2.ap()[0:T, :], reads=[kn2])
                else:
                    qdst = qb if is_p else s_qb[si]
                    op("pool", lambda e: e.tensor_tensor(qdst.ap().rearrange("p (h d) -> p h d", h=8), kn.ap().rearrange("p (h d) -> p h d", h=8),
                                                         qw_b.ap().unsqueeze(1).to_broadcast([128, 8, 64]), ALU.mult), reads=[kn, qw_b], writes=[qdst])

            z0 = proj_tok(1, C_Z, 512)
            z1 = proj_tok(2, C_Z + 512, 512)
            op("act", lambda e: e.activation(out=zs.ap()[:, 0:512], in_=z0, func=AF.Silu), reads=[banks[1]], writes=[zs])
            qp = proj_tok(1, C_Q, 512)
            op("act", lambda e: e.activation(out=zs.ap()[:, 512:1024], in_=z1, func=AF.Silu), reads=[banks[2]], writes=[zs])
            kp = proj_tok(2, C_K, 512)
            qk_norm("q", qp, 1)
            vp = proj_tok(1, C_V, 512)
            qk_norm("k", kp, 2)
            op("act", lambda e: e.activation(out=vtok.ap(), in_=vp, func=AF.Copy), reads=[banks[1]], writes=[vtok])
            vdst = vb_t if is_p else s_vb[si]
            op("pool", lambda e: e.tensor_copy(vdst.ap()[:, :, 0:64], vtok.ap().rearrange("p (h d) -> p h d", h=8)), reads=[vtok], writes=[vdst])
            if is_p:
                P.dma("sp", vS[:, ti, :, :].rearrange("h s d -> s h d"), vb_t.ap(), reads=[vb_t])
                P.dma("sp", p_v[ti * 128:(ti + 1) * 128, :], vtok.ap(), reads=[vtok])
            else:
                P.dma("sp", s_v[si], vtok.ap()[0:T, :], reads=[vtok])
            dtp = proj_tok(0, C_DT, 16, off=0)
            fp = proj_tok(0, C_F, 8, off=16)
            fTp = bview(0, F32, [128], parts=8, off=32)
            for kc in range(8):
                op("pe", lambda e, kc=kc: e.matmul(fTp, w_in_t[kc].ap()[:, C_F:C_F + 8], hT.ap()[:, kc, :], start=(kc == 0), stop=(kc == 7)),
                   reads=[hT, w_in_t[kc]], writes=[banks[0]])
            chk(2)
            if is_p:
                if ti == 0:
                    op("pool", lambda e: e.memset(xp.ap()[:, :, 0:3], 0.0), writes=[xp])
                else:
                    op("pool", lambda e: e.tensor_copy(xp.ap()[:, :, 0:3], xpo.ap()[:, :, 128:131]), reads=[xpo], writes=[xp])
            else:
                for k in range(3):
                    P.dma("sp", cst.ap()[:, :, k], state_conv[si, k].rearrange("(c p) -> p c", p=128), writes=[cst], allow_slow_non_contiguous=True)
                op("pool", lambda e: e.tensor_copy(xp.ap()[:, :, 0:3], cst.ap()), reads=[cst], writes=[xp])
            chk(5)
            op("dve", lambda e: e.tensor_tensor(lft.ap(), fp, fb_b.ap(), ALU.add), reads=[banks[0], fb_b], writes=[lft])
            op("act", lambda e: e.activation(out=lft.ap(), in_=lft.ap(), func=AF.Exp, scale=-1.0), reads=[lft], writes=[lft])
            op("act", lambda e: e.activation(out=lft.ap(), in_=lft.ap(), func=AF.Ln, bias=1.0), reads=[lft], writes=[lft])
            lfdst = lfa if is_p else s_lf[si]
            op("dve", lambda e: e.tensor_scalar_mul(lfdst.ap(), lft.ap(), -1.0), reads=[lft], writes=[lfdst])
            if is_p:
                P.dma("sp", p_logf[ti * 128:(ti + 1) * 128, :], lfdst.ap(), reads=[lfdst])
            else:
                P.dma("sp", s_logf[si], lfdst.ap()[0:T, :], reads=[lfdst])
            chk(6)
            if is_p:
                op("act", lambda e: e.activation(out=fT1.ap(), in_=fTp, func=AF.Exp, scale=-1.0, bias=negfb.ap()), reads=[banks[0], negfb], writes=[fT1])
                op("act", lambda e: e.activation(out=fT1.ap(), in_=fT1.ap(), func=AF.Ln, bias=1.0), reads=[fT1], writes=[fT1])
                op("dve", lambda e: e.tensor_scalar_mul(fT2.ap(), fT1.ap(), -1.0), reads=[fT1], writes=[fT2])
                if ti == 0:
                    op("dve", lambda e: e.tensor_tensor_scan(cT.ap(), ones8.ap(), fT2.ap(), 0.0, ALU.mult, ALU.add), reads=[ones8, fT2], writes=[cT])
                else:
                    op("dve", lambda e: e.tensor_tensor_scan(cT.ap(), ones8.ap(), fT2.ap(), ccar.ap(), ALU.mult, ALU.add), reads=[ones8, fT2, ccar], writes=[cT])
                op("dve", lambda e: e.tensor_copy(ccar.ap(), cT.ap()[:, 127:128]), reads=[cT], writes=[ccar])
                op("pool", lambda e: e.tensor_copy(csp.ap()[:, 0, :], cT.ap()), reads=[cT], writes=[csp])
                op("pool", lambda e: e.tensor_tensor(cr.ap(), cT.ap(), csp.ap()[:, 0, :], ALU.subtract), reads=[cT, csp], writes=[cr])
                op("pool", lambda e: e.tensor_copy(csp.ap()[:, 1, :], cr.ap()), reads=[cr], writes=[csp])
                op("pool", lambda e: e.tensor_tensor(cr.ap(), cr.ap(), csp.ap()[:, 1, :], ALU.subtract), reads=[cr, csp], writes=[cr])
                op("pool", lambda e: e.tensor_copy(csp.ap()[:, 2, :], cr.ap()), reads=[cr], writes=[csp])
                op("pool", lambda e: e.tensor_scalar_mul(csp.ap()[:, 3:6, :], csp.ap()[:, 0:3, :], -1.0), reads=[csp], writes=[csp])
                P.dma("sp", qT_s[:, 64:67, ti * 128:(ti + 1) * 128], csp.ap()[:, 0:3, :], reads=[csp])
                P.dma("sp", kT_s[:, 67:70, ti * 128:(ti + 1) * 128], csp.ap()[:, 3:6, :], reads=[csp])
                qkp = bview(2, BF16, [8, 128], parts=64)
                for src_t, which, dstT in ((qb, 0, qT_s), (kb, 1, kT_s)):
                    for h in range(8):
                        op("pe", lambda e, h=h, src_t=src_t: e.transpose(qkp[:, h, :], src_t.ap()[:, h * 64:(h + 1) * 64], identb.ap()),
                           reads=[src_t, identb], writes=[banks[2]])
                    op("dve", lambda e, which=which: e.tensor_copy(qkT.ap()[:, 0, :, :], qkp), reads=[banks[2]], writes=[qkT])
                    P.dma("sp", dstT[:, 0:64, ti * 128:(ti + 1) * 128].rearrange("h d t -> d h t"), qkT.ap()[:, 0, :, :], reads=[qkT])
            chk(7)
            op("dve", lambda e: e.tensor_tensor(dtt.ap(), dtp, dtb_b.ap(), ALU.add), reads=[banks[0], dtb_b], writes=[dtt])
            op("act", lambda e: e.activation(out=dtt.ap(), in_=dtt.ap(), func=AF.Exp), reads=[dtt], writes=[dtt])
            op("act", lambda e: e.activation(out=dt_.ap(), in_=dtt.ap(), func=AF.Ln, bias=1.0), reads=[dtt], writes=[dt_])
            op("dve", lambda e: e.tensor_tensor(dA.ap(), dt_.ap(), negA_b.ap(), ALU.mult), reads=[dt_, negA_b], writes=[dA])
            for grp in range(3):
                xbk = (3, 0, 3)[grp]
                xb = bview(xbk, F32, [4, 128])
                for j in range(4):
                    cc = grp * 4 + j
                    for kc in range(8):
                        op("pe", lambda e, kc=kc, cc=cc, j=j: e.matmul(xb[:, j, :], w_in_t[kc].ap()[:, C_XBC + cc * 128:C_XBC + (cc + 1) * 128],
                                                                        hT.ap()[:, kc, :], start=(kc == 0), stop=(kc == 7)),
                           reads=[hT, w_in_t[kc]], writes=[banks[xbk]])
                op("act", lambda e, grp=grp: e.activation(out=xp.ap()[:, grp * 4:(grp + 1) * 4, 3:131], in_=xb, func=AF.Copy), reads=[banks[xbk]], writes=[xp])
            chk(3)
            chk(8)
            for grp in range(3):
                cvk = (0, 3, 0)[grp]
                cvp = bview(cvk, F32, [4, 128])
                for j in range(4):
                    cc = grp * 4 + j
                    for k in range(5):
                        rhs_ap = xp.ap()[:, cc, k:k + 128] if k < 4 else ones_b.ap()[:, 0:128]
                        op("pe", lambda e: e.matmul(cvp[:, j, :], wd.ap()[:, cc, k, :], rhs_ap, start=(k == 0), stop=(k == 4)),
                           reads=[wd, xp, ones_b], writes=[banks[cvk]])
                op("act", lambda e: e.activation(out=xc.ap()[:, grp * 4:(grp + 1) * 4, :], in_=cvp, func=AF.Silu), reads=[banks[cvk]], writes=[xc])
            if ti == NT - 1:
                op("pool", lambda e: e.tensor_copy(cso.ap(), xp.ap()[:, :, 128:131]), reads=[xp], writes=[cso])
                for k in range(3):
                    P.dma("sp", p_conv[k].rearrange("(c p) -> p c", p=128), cso.ap()[:, :, k], reads=[cso], allow_slow_non_contiguous=True)
            if not is_p:
                op("pool", lambda e: e.tensor_copy(cso.ap(), xp.ap()[:, :, T:T + 3]), reads=[xp], writes=[cso])
                for k in range(3):
                    P.dma("sp", s_conv[si, k].rearrange("(c p) -> p c", p=128), cso.ap()[:, :, k], reads=[cso], allow_slow_non_contiguous=True)
            chk(9)
            xtp = bview(4, BF16, [16, 64])
            xtp2 = bview(4, BF16, [8, 128])
            for c in range(8):
                op("pe", lambda e, c=c: e.transpose(xtp2[:, c, :], xc.ap()[:, c, :], identb.ap()), reads=[xc, identb], writes=[banks[4]])
            chk(91)
            op("dve", lambda e: e.tensor_tensor(xdt.ap().rearrange("p (h d) -> p h d", h=16), xtp, dt_.ap().unsqueeze(2).to_broadcast([128, 16, 64]), ALU.mult),
               reads=[banks[4], dt_], writes=[xdt])
            chk(92)
            op("dve", lambda e: e.tensor_copy(xtok.ap().rearrange("p (c t) -> p c t", c=8), xtp2), reads=[banks[4]], writes=[xtok])
            chk(93)
            btp = bview(5, BF16, [2, 128], off=512)
            for g in range(2):
                op("pe", lambda e, g=g: e.transpose(btp[:, g, :], xc.ap()[:, 8 + g, :], identb.ap()), reads=[xc, identb], writes=[banks[5]])
            chk(94)
            op("dve", lambda e: e.tensor_copy(Btok.ap(), btp), reads=[banks[5]], writes=[Btok])
            chk(10)
            sm = bview(5, F32, [3, 16], off=384)
            op("pe", lambda e: e.matmul(sm[:, 0, :], LE_f.ap()[0:L, :], dA.ap()[0:L, :], start=True, stop=True), reads=[LE_f, dA], writes=[banks[5]])
            op("pe", lambda e: e.matmul(sm[:, 1, :], GT_f.ap()[0:L, :], dA.ap()[0:L, :], start=True, stop=True), reads=[GT_f, dA], writes=[banks[5]])
            op("pe", lambda e: e.matmul(sm[:, 2, :], ones_f.ap()[0:L, :], dA.ap()[0:L, :], start=True, stop=True), reads=[ones_f, dA], writes=[banks[5]])
            op("act", lambda e: e.activation(out=eacs.ap(), in_=sm[:, 0, :], func=AF.Exp), reads=[banks[5]], writes=[eacs])
            op("act", lambda e: e.activation(out=tail.ap(), in_=sm[:, 1, :], func=AF.Exp), reads=[banks[5]], writes=[tail])
            op("act", lambda e: e.activation(out=cd_b.ap(), in_=sm[:, 2, :], func=AF.Exp), reads=[banks[5]], writes=[cd_b])
            op("dve", lambda e: e.tensor_tensor(dtail.ap(), dt_.ap(), tail.ap(), ALU.mult), reads=[dt_, tail], writes=[dtail])
            op("dve", lambda e: e.tensor_tensor(xdtt.ap().rearrange("p (h d) -> p h d", h=16), xtp, dtail.ap().unsqueeze(2).to_broadcast([128, 16, 64]), ALU.mult),
               reads=[banks[4], dtail], writes=[xdtt])
            chk(11)
            cbp = bview(5, F32, [2, 128], off=0)
            for g in range(2):
                op("pe", lambda e, g=g: e.matmul(cbp[:, g, :], xc.ap()[:, 8 + g, :], xc.ap()[:, 10 + g, :], start=True, stop=True), reads=[xc], writes=[banks[5]])
            op("dve", lambda e: e.tensor_tensor(cbm.ap(), cbp, LE_f.ap().unsqueeze(1).to_broadcast([128, 2, 128]), ALU.mult), reads=[banks[5], LE_f], writes=[cbm])
            chk(12)
            op("pool", lambda e: e.tensor_copy(dAh.ap(), dA.ap()), reads=[dA], writes=[dAh])
            op("pool", lambda e: e.tensor_tensor(dAl.ap(), dA.ap(), dAh.ap(), ALU.subtract), reads=[dA, dAh], writes=[dAl])
            op("pool", lambda e: e.tensor_tensor(rseg_h.ap(), LE_f.ap().unsqueeze(1).to_broadcast([128, 16, 128]), dAh.ap().unsqueeze(2).to_broadcast([128, 16, 128]), ALU.mult),
               reads=[LE_f, dAh], writes=[rseg_h])
            op("dve", lambda e: e.tensor_tensor(rseg_l.ap(), LE_f.ap().unsqueeze(1).to_broadcast([128, 16, 128]), dAl.ap().unsqueeze(2).to_broadcast([128, 16, 128]), ALU.mult),
               reads=[LE_f, dAl], writes=[rseg_l])
            for g in range(2):
                for q4 in range(2):
                    sg = bview(6 + q4, F32, [4, 128])
                    op("pe", lambda e, g=g, q4=q4, sg=sg: e.matmul(sg, GT_b.ap()[0:L, :], rseg_h.ap()[0:L, g * 8 + q4 * 4:g * 8 + q4 * 4 + 4, :], start=True, stop=False),
                       reads=[GT_b, rseg_h], writes=[banks[6 + q4]])
                    op("pe", lambda e, g=g, q4=q4, sg=sg: e.matmul(sg, GT_b.ap()[0:L, :], rseg_l.ap()[0:L, g * 8 + q4 * 4:g * 8 + q4 * 4 + 4, :], start=False, stop=True),
                       reads=[GT_b, rseg_l], writes=[banks[6 + q4]])
                    op("act", lambda e, q4=q4, sg=sg: e.activation(out=decT.ap()[:, q4 * 4:(q4 + 1) * 4, :], in_=sg, func=AF.Exp), reads=[banks[6 + q4]], writes=[decT])
                op("dve" if g == 0 else "pool", lambda e, g=g: e.tensor_tensor(MT.ap()[:, g * 8:(g + 1) * 8, :], decT.ap(), cbm.ap()[:, g:g + 1, :].to_broadcast([128, 8, 128]), ALU.mult),
                   reads=[decT, cbm], writes=[MT])

        def p1_B(ti):
            par, zs, xc, xdt, xtok, xdtt, Btok, MT, eacs, cd_b = p1_vars(ti)
            is_p = ti < NT
            L = 128 if is_p else T
            si = ti - NT
            if ti == 0:
                op("pool", lambda e: e.memset(stT.ap(), 0.0), writes=[stT])
                op("pool", lambda e: e.memset(stT_b.ap(), 0.0), writes=[stT_b])
            if not is_p:
                P.dma("sp", st_out.ap(), state_ssm[si].rearrange("(c p) n -> p c n", p=128), writes=[st_out])
                for half in range(2):
                    stp = bview(4 + half, F32, [4, 128])
                    for j in range(4):
                        op("pe", lambda e, j=j, half=half, stp=stp: e.transpose(stp[:, j, :], st_out.ap()[:, half * 4 + j, :], identf.ap()),
                           reads=[st_out, identf], writes=[banks[4 + half]])
                    op("dve", lambda e, half=half, stp=stp: e.tensor_copy(stT.ap()[:, half * 512:(half + 1) * 512], stp.rearrange("p a b -> p (a b)")),
                       reads=[banks[4 + half]], writes=[stT])
                op("act", lambda e: e.activation(out=stT_b.ap(), in_=stT.ap(), func=AF.Copy), reads=[stT], writes=[stT_b])
            chk(14)
            yb = [bview(6, F32, [8, 64]), bview(7, F32, [8, 64])]
            for h in range(16):
                op("pe", lambda e, h=h: e.matmul(yb[h // 8][:, h % 8, :], MT.ap()[0:L, h, :], xdt.ap()[0:L, h * 64:(h + 1) * 64], start=True, stop=False),
                   reads=[MT, xdt], writes=[banks[6 + h // 8]])
                op("pe", lambda e, h=h: e.matmul(yb[h // 8][:, h % 8, :], Dd.ap()[0:L, h, :], xtok.ap()[0:L, h * 64:(h + 1) * 64], start=False, stop=True),
                   reads=[Dd, xtok], writes=[banks[6 + h // 8]])
            for g in range(2):
                tb = bview(4 + g, F32, [512])
                op("pe", lambda e, g=g, tb=tb: e.matmul(tb, xc.ap()[:, 10 + g, :], stT_b.ap()[:, g * 512:(g + 1) * 512], start=True, stop=True),
                   reads=[xc, stT_b], writes=[banks[4 + g]])
                op("dve", lambda e, g=g, tb=tb: e.tensor_tensor(t2.ap()[:, g * 512:(g + 1) * 512].rearrange("p (h d) -> p h d", h=8), tb.rearrange("p (h d) -> p h d", h=8),
                                                                 eacs.ap()[:, g * 8:(g + 1) * 8].unsqueeze(2).to_broadcast([128, 8, 64]), ALU.mult),
                   reads=[banks[4 + g], eacs], writes=[t2])
            chk(15)
            for g in range(2):
                ib = bview(4 + g, F32, [512])
                op("pe", lambda e, g=g, ib=ib: e.matmul(ib, Btok.ap()[0:L, g, :], xdtt.ap()[0:L, g * 512:(g + 1) * 512], start=True, stop=True),
                   reads=[Btok, xdtt], writes=[banks[4 + g]])
            op("pool", lambda e: e.tensor_tensor(st_tmp.ap().rearrange("p (h d) -> p h d", h=16), stT.ap().rearrange("p (h d) -> p h d", h=16),
                                                 cd_b.ap().unsqueeze(2).to_broadcast([128, 16, 64]), ALU.mult), reads=[stT, cd_b], writes=[st_tmp])
            for g in range(2):
                ib = bview(4 + g, F32, [512])
                op("dve", lambda e, g=g, ib=ib: e.tensor_tensor(stT.ap()[:, g * 512:(g + 1) * 512], st_tmp.ap()[:, g * 512:(g + 1) * 512], ib, ALU.add),
                   reads=[st_tmp, banks[4 + g]], writes=[stT])
            op("act", lambda e: e.activation(out=stT_b.ap(), in_=stT.ap(), func=AF.Copy), reads=[stT], writes=[stT_b])
            if ti == NT - 1 or not is_p:
                for half in range(2):
                    stp = bview(4 + half, F32, [4, 128])
                    for j in range(4):
                        c = half * 4 + j
                        op("pe", lambda e, j=j, c=c, stp=stp: e.transpose(stp[:, j, :], stT.ap()[:, c * 128:(c + 1) * 128], identf.ap()),
                           reads=[stT, identf], writes=[banks[4 + half]])
                    op("dve", lambda e, half=half, stp=stp: e.tensor_copy(st_out.ap()[:, half * 4:(half + 1) * 4, :], stp), reads=[banks[4 + half]], writes=[st_out])
                dst = p_ssm if is_p else s_ssm[si]
                P.dma("sp", dst.rearrange("(c p) n -> p c n", p=128), st_out.ap(), reads=[st_out])
            chk(16)
            for g in range(2):
                yv = bview(6 + g, F32, [512])
                op("dve", lambda e, g=g, yv=yv: e.tensor_tensor(t2.ap()[:, g * 512:(g + 1) * 512], yv, t2.ap()[:, g * 512:(g + 1) * 512], ALU.add),
                   reads=[banks[6 + g], t2], writes=[t2])
            op("pool", lambda e: e.tensor_tensor(gsb.ap(), t2.ap(), zs.ap(), ALU.mult), reads=[t2, zs], writes=[gsb])
            for g in range(2):
                op("act", lambda e, g=g: e.activation(out=junk.ap()[:, g * 512:(g + 1) * 512], in_=gsb.ap()[:, g * 512:(g + 1) * 512], func=AF.Square,
                                                       accum_out=ssg.ap()[:, g:g + 1]), reads=[gsb], writes=[ssg])
            rstd_from(ssg.ap()[:, 0:2], ssg.ap()[:, 2:4], ssg.ap()[:, 0:2], 512, [ssg])
            for g in range(2):
                op("act", lambda e, g=g: e.activation(out=yn.ap()[:, g * 512:(g + 1) * 512], in_=gsb.ap()[:, g * 512:(g + 1) * 512], func=AF.Copy,
                                                       scale=ssg.ap()[:, 2 + g:3 + g]), reads=[gsb, ssg], writes=[yn])
            P.dma("sp", yn_s[ti * 128:(ti + 1) * 128, :], yn.ap(), reads=[yn])

        if STOP == 1:
            raise _Stop()
        load_x(0, 0)
        p1_A(0)
        for ti in range(NTT):
            if ti + 1 < NTT:
                p1_A(ti + 1)
            p1_B(ti)
        P.pop()
        if STOP == 2:
            raise _Stop()

        P.push()
        maskb = P.sb("maskb", [128, 4, 512], BF16)
        op("pool", lambda e: e.memset(maskb.ap(), 0.0), writes=[maskb])
        for j in range(4):
            op("pool", lambda e, j=j: e.affine_select(out=maskb.ap()[:, j, :], in_=maskb.ap()[:, j, :], pattern=[[1, 512]], compare_op=ALU.is_ge,
                                                      fill=NEG, base=-128 * j, channel_multiplier=-1), reads=[maskb], writes=[maskb])
        Vh = [P.sb("Vh%d" % i, [128, NT, 128], BF16) for i in range(2)]
        KT = [P.sb("KT%d" % i, [70, S], BF16) for i in range(2)]
        QT = [P.sb("QT%d" % i, [70, S], BF16) for i in range(2)]
        PT = [P.sb("PT%d" % i, [128, 2, 512], BF16) for i in range(4)]
        osb = [P.sb("osb%d" % i, [128, 512], F32) for i in range(2)]
        recb = [P.sb("recb%d" % i, [64, 512], F32) for i in range(2)]
        yf = [P.sb("yf%d" % i, [64, 512], BF16) for i in range(2)]

        def load_head(h):
            hp = h % 2
            P.dma("sp", Vh[hp].ap(), vS[h].rearrange("b s d -> s b d"), writes=[Vh[hp]])
            for c0 in range(0, S, 2048):
                c1 = min(S, c0 + 2048)
                P.dma("sp", KT[hp].ap()[:, c0:c1], kT_s[h, :, c0:c1], writes=[KT[hp]])
                P.dma("sp", QT[hp].ap()[:, c0:c1], qT_s[h, :, c0:c1], writes=[QT[hp]])

        jobs = []
        gidx = 0
        for h in range(8):
            for qg in range(NQG):
                nkb = 4 * qg + 4
                for kb0 in range(0, nkb, 2):
                    jobs.append((h, qg, kb0, nkb, gidx))
                gidx += 1
        loaded = set()

        def ensure_head(h):
            if h < 8 and h not in loaded:
                loaded.add(h)
                load_head(h)

        def emit_qk(ji):
            h, qg, kb0, nkb, g = jobs[ji]
            hp = h % 2
            sp_ = ji % 2
            for j in range(2):
                kbi = kb0 + j
                sc = bview(2 * sp_ + j, F32, [512])
                diag = kbi >= 4 * qg
                c0_ = 128 * (kbi - 4 * qg) if diag else 0
                op("pe", lambda e: e.matmul(sc[:, c0_:512], KT[hp].ap()[:, kbi * 128:(kbi + 1) * 128], QT[hp].ap()[:, qg * 512 + c0_:(qg + 1) * 512],
                                            start=True, stop=(not diag)), reads=[KT[hp], QT[hp]], writes=[banks[2 * sp_ + j]])
                if diag:
                    jj = kbi - 4 * qg
                    op("pe", lambda e: e.matmul(sc[:, c0_:512], identb.ap(), maskb.ap()[:, jj, c0_:512], start=False, stop=True),
                       reads=[identb, maskb], writes=[banks[2 * sp_ + j]])

        def emit_exp_pv(ji):
            h, qg, kb0, nkb, g = jobs[ji]
            hp = h % 2
            sp_ = ji % 2
            oi = 4
            oacc = bview(oi, F32, [512], parts=128)
            sc2 = P.ps_arena[0:128, (2 * sp_) * 2048:(2 * sp_ + 2) * 2048].bitcast(F32).rearrange("p (a b) -> p a b", a=2)
            op("act", lambda e: e.activation(out=PT[ji % 4].ap(), in_=sc2, func=AF.Exp),
               reads=[banks[2 * sp_], banks[2 * sp_ + 1]], writes=[PT[ji % 4]])
            for j in range(2):
                kbi = kb0 + j
                c0_ = 128 * (kbi - 4 * qg) if kbi >= 4 * qg else 0
                op("pe", lambda e: e.matmul(oacc[:, c0_:512], Vh[hp].ap()[:, kbi, :], PT[ji % 4].ap()[:, j, c0_:512], start=(kbi == 0), stop=(kbi == nkb - 1)),
                   reads=[Vh[hp], PT[ji % 4]], writes=[banks[oi]])

        def emit_epi1(ji):
            h, qg, kb0, nkb, g = jobs[ji]
            oi = 4
            op_ = g % 2
            oacc = bview(oi, F32, [512], parts=128)
            op("dve", lambda e: e.tensor_copy(osb[op_].ap(), oacc), reads=[banks[oi]], writes=[osb[op_]])
            op("pool", lambda e: e.tensor_copy(recb[op_].ap(), osb[op_].ap()[64:128, :]), reads=[osb[op_]], writes=[recb[op_]])
            op("dve", lambda e: e.reciprocal(recb[op_].ap(), recb[op_].ap()), reads=[recb[op_]], writes=[recb[op_]])
            op("pool", lambda e: e.tensor_tensor(yf[op_].ap(), osb[op_].ap()[0:64, :], recb[op_].ap(), ALU.mult), reads=[osb[op_], recb[op_]], writes=[yf[op_]])
            P.dma("sp", yfT_s[h * 64:(h + 1) * 64, qg * 512:(qg + 1) * 512], yf[op_].ap(), reads=[yf[op_]])

        def emit_epi2(ji):
            pass

        ensure_head(0)
        emit_qk(0)
        pending = None
        for ji in range(len(jobs)):
            h, qg, kb0, nkb, g = jobs[ji]
            if kb0 == 0 and qg == 0:
                ensure_head(h + 1)
            if ji + 1 < len(jobs):
                emit_qk(ji + 1)
            emit_exp_pv(ji)
            if pending is not None:
                emit_epi2(pending)
                pending = None
            if kb0 + 2 >= nkb:
                emit_epi1(ji)
                pending = ji
        if pending is not None:
            emit_epi2(pending)
        wst = [P.sb("wst%d" % i, [128, 512], F32) for i in range(2)]
        wbf = [P.sb("wbf%d" % i, [128, 512], BF16) for i in range(2)]
        wi_ = 0
        for src, dst, nk, ncols, sc_t, sc_nk in ((w_out, wo_s, 12, D, gn, 8), (w_up, wu_s, 8, 4096, g2, 8), (w_down, wdn_s, 32, D, None, 0)):
            for kc in range(nk):
                for c0 in range(0, ncols, 512):
                    bi = wi_ % 2
                    eng_ = "pool" if wi_ % 3 == 2 else "dve"
                    wi_ += 1
                    P.dma("sp", wst[bi].ap(), src[kc * 128:(kc + 1) * 128, c0:c0 + 512], writes=[wst[bi]])
                    if sc_t is not None and kc < sc_nk:
                        op(eng_, lambda e: e.tensor_scalar_mul(wbf[bi].ap(), wst[bi].ap(), sc_t.ap()[:, kc:kc + 1]), reads=[wst[bi], sc_t], writes=[wbf[bi]])
                    else:
                        op(eng_, lambda e: e.tensor_copy(wbf[bi].ap(), wst[bi].ap()), reads=[wst[bi]], writes=[wbf[bi]])
                    P.dma("sp", dst[kc * 128:(kc + 1) * 128, c0:c0 + 512], wbf[bi].ap(), reads=[wbf[bi]])
        if STOP == 3:
            raise _Stop()
        G = 4 if NJ % 4 == 0 else (2 if NJ % 2 == 0 else 1)
        lf = P.sb("lf", [128, NJ, 8], F32)
        Wt = P.sb("Wt", [128, NJ, 8], F32)
        Tt = P.sb("Tt", [128, 8], F32)
        nb = P.sb("nb", [128, NJ, 8], F32)
        nbn = P.sb("nbn", [16, 8], F32)
        cmask = P.sb("cmask", [16, 16], F32)
        Qbd = P.sb("Qbd", [128, 4, 32], BF16)
        kc_f = [P.sb("kc_f%d" % i, [128, G, 512], F32) for i in range(2)]
        vc_f = [P.sb("vc_f%d" % i, [128, G, 512], F32) for i in range(2)]
        kc_b = [P.sb("kc_b%d" % i, [128, G, 512], BF16) for i in range(2)]
        vc_b = [P.sb("vc_b%d" % i, [128, G, 8, 65], BF16) for i in range(2)]
        k2T = [P.sb("k2T%d" % i, [128, 4, 128], BF16) for i in range(2)]
        sbs = [P.sb("sbs%d" % i, [128, 8, 16], F32) for i in range(2)]
        pTs = [P.sb("pTs%d" % i, [128, 8, 16], BF16) for i in range(2)]
        osb_s = P.sb("osb_s", [16, 8, 65], F32)
        rec_s = P.sb("rec_s", [16, 8, 1], F32)
        yf_s = P.sb("yf_s", [16, 512], BF16)
        yfT_sb = P.sb("yfT_sb", [128, 4, 16], BF16)
        for i in range(2):
            op("pool", lambda e, i=i: e.memset(vc_b[i].ap(), 1.0), writes=[vc_b[i]])
        op("pool", lambda e: e.memset(cmask.ap(), 0.0), writes=[cmask])
        op("pool", lambda e: e.affine_select(out=cmask.ap(), in_=cmask.ap(), pattern=[[1, 16]], compare_op=ALU.is_ge, fill=NEG, base=0, channel_multiplier=-1),
           reads=[cmask], writes=[cmask])

        def load_cache(st, g0, par):
            P.dma("sp", kc_f[par].ap(), cache_k[st].rearrange("(p j) c -> p j c", j=NJ)[:, g0:g0 + G, :], writes=[kc_f[par]])
            P.dma("sp", vc_f[par].ap(), cache_v[st].rearrange("(p j) c -> p j c", j=NJ)[:, g0:g0 + G, :], writes=[vc_f[par]])

        gi = 0
        for st in range(NSTREAM):
            load_cache(st, 0, gi % 2)
            P.dma("sp", lf.ap(), cache_logf[st].rearrange("(p j) h -> p j h", j=NJ), writes=[lf])
            op("pool", lambda e: e.memset(Wt.ap()[:, NJ - 1, :], 0.0), writes=[Wt])
            for j in range(NJ - 2, -1, -1):
                op("pool", lambda e, j=j: e.tensor_tensor(Wt.ap()[:, j, :], Wt.ap()[:, j + 1, :], lf.ap()[:, j + 1, :], ALU.add), reads=[Wt, lf], writes=[Wt])
            op("pool", lambda e: e.tensor_tensor(Tt.ap(), Wt.ap()[:, 0, :], lf.ap()[:, 0, :], ALU.add), reads=[Wt, lf], writes=[Tt])
            xps = bview(6, F32, [8], off=384)
            op("pe", lambda e: e.matmul(xps, GT_f.ap(), Tt.ap(), start=True, stop=True), reads=[GT_f, Tt], writes=[banks[6]])
            op("dve", lambda e: e.tensor_tensor(nb.ap(), Wt.ap(), xps.unsqueeze(1).to_broadcast([128, NJ, 8]), ALU.add), reads=[Wt, banks[6]], writes=[nb])
            dps = bview(6, F32, [8], parts=16, off=392)
            op("pe", lambda e: e.matmul(dps, LE_f.ap()[0:16, 0:16], s_lf[st].ap()[0:16, :], start=True, stop=True), reads=[LE_f, s_lf[st]], writes=[banks[6]])
            op("dve", lambda e: e.tensor_scalar_mul(nbn.ap(), dps, -1.0), reads=[banks[6]], writes=[nbn])
            op("pool", lambda e: e.memset(Qbd.ap(), 0.0), writes=[Qbd])
            qtp = bview(6, BF16, [4, 16], off=800)
            for pr in range(4):
                op("pe", lambda e, pr=pr: e.transpose(qtp[:, pr, :], s_qb[st].ap()[0:16, pr * 128:(pr + 1) * 128], identb.ap()[0:16, 0:16]),
                   reads=[s_qb[st], identb], writes=[banks[6]])
            op("dve", lambda e: e.tensor_copy(Qbd.ap()[0:64, :, 0:16], qtp[0:64, :, :]), reads=[banks[6]], writes=[Qbd])
            op("dve", lambda e: e.tensor_copy(Qbd.ap()[64:128, :, 16:32], qtp[64:128, :, :]), reads=[banks[6]], writes=[Qbd])
            oA = bview(5, F32, [4, 65], parts=16)
            oB = bview(7, F32, [4, 65], parts=16)
            blk = 0
            for g0 in range(0, NJ, G):
                par = gi % 2
                gi += 1
                if g0 + G < NJ:
                    load_cache(st, g0 + G, gi % 2)
                elif st + 1 < NSTREAM:
                    pass
                op("dve", lambda e, par=par: e.tensor_copy(kc_b[par].ap(), kc_f[par].ap()), reads=[kc_f[par]], writes=[kc_b[par]])
                op("pool", lambda e, par=par: e.tensor_copy(vc_b[par].ap()[:, :, :, 0:64], vc_f[par].ap().rearrange("p g (h d) -> p g h d", h=8)),
                   reads=[vc_f[par]], writes=[vc_b[par]])
                for jj in range(G):
                    j = g0 + jj
                    bp = blk % 2
                    blk += 1
                    ktp = bview(6, BF16, [4, 128])
                    for pr in range(4):
                        op("pe", lambda e, pr=pr, jj=jj, par=par, ktp=ktp: e.transpose(ktp[:, pr, :], kc_b[par].ap()[:, jj, pr * 128:(pr + 1) * 128], identb.ap()),
                           reads=[kc_b[par], identb], writes=[banks[6]])
                    op("dve", lambda e, bp=bp, ktp=ktp: e.tensor_copy(k2T[bp].ap(), ktp), reads=[banks[6]], writes=[k2T[bp]])
                    scs = bview(6, F32, [4, 32], off=256)
                    for pr in range(4):
                        op("pe", lambda e, pr=pr, bp=bp, scs=scs: e.matmul(scs[:, pr, :], k2T[bp].ap()[:, pr, :], Qbd.ap()[:, pr, :], start=True, stop=True),
                           reads=[k2T[bp], Qbd], writes=[banks[6]])
                    op("dve", lambda e, bp=bp, j=j, scs=scs: e.tensor_tensor(sbs[bp].ap(), scs.rearrange("p a (b t) -> p (a b) t", b=2),
                                                                             nb.ap()[:, j, :].unsqueeze(2).to_broadcast([128, 8, 16]), ALU.add),
                       reads=[banks[6], nb], writes=[sbs[bp]])
                    op("act", lambda e, bp=bp: e.activation(out=pTs[bp].ap(), in_=sbs[bp].ap(), func=AF.Exp), reads=[sbs[bp]], writes=[pTs[bp]])
                    for hh in range(8):
                        o_ = (oA if hh < 4 else oB)[:, hh % 4, :]
                        op("pe", lambda e, hh=hh, bp=bp, jj=jj, par=par, o_=o_, j=j: e.matmul(o_, pTs[bp].ap()[:, hh, :], vc_b[par].ap()[:, jj, hh, :], start=(j == 0 and hh % 4 == 0), stop=False, skip_group_check=True),
                           reads=[pTs[bp], vc_b[par]], writes=[banks[5 + 2 * (hh // 4)]])
            ktp = bview(6, BF16, [4, 16])
            for pr in range(4):
                op("pe", lambda e, pr=pr: e.transpose(ktp[:, pr, :], s_kb[st].ap()[0:16, pr * 128:(pr + 1) * 128], identb.ap()[0:16, 0:16]),
                   reads=[s_kb[st], identb], writes=[banks[6]])
            op("dve", lambda e: e.tensor_copy(k2T[0].ap()[:, :, 0:16], ktp), reads=[banks[6]], writes=[k2T[0]])
            scs = bview(6, F32, [4, 32], parts=16, off=256)
            for pr in range(4):
                op("pe", lambda e, pr=pr: e.matmul(scs[:, pr, :], k2T[0].ap()[:, pr, 0:16], Qbd.ap()[:, pr, :], start=True, stop=True), reads=[k2T[0], Qbd], writes=[banks[6]])
            op("dve", lambda e: e.tensor_tensor(sbs[0].ap()[0:16], scs.rearrange("p a (b t) -> p (a b) t", b=2), nbn.ap().unsqueeze(2).to_broadcast([16, 8, 16]), ALU.add),
               reads=[banks[6], nbn], writes=[sbs[0]])
            op("dve", lambda e: e.tensor_tensor(sbs[0].ap()[0:16], sbs[0].ap()[0:16], cmask.ap().unsqueeze(1).to_broadcast([16, 8, 16]), ALU.add),
               reads=[sbs[0], cmask], writes=[sbs[0]])
            op("act", lambda e: e.activation(out=pTs[0].ap()[0:16], in_=sbs[0].ap()[0:16], func=AF.Exp), reads=[sbs[0]], writes=[pTs[0]])
            for hh in range(8):
                o_ = (oA if hh < 4 else oB)[:, hh % 4, :]
                op("pe", lambda e, hh=hh, o_=o_: e.matmul(o_, pTs[0].ap()[0:16, hh, :], s_vb[st].ap()[0:16, hh, :], start=False, stop=True, skip_group_check=True),
                   reads=[pTs[0], s_vb[st]], writes=[banks[5 + 2 * (hh // 4)]])
            op("dve", lambda e: e.tensor_copy(osb_s.ap()[:, 0:4, :], oA), reads=[banks[5]], writes=[osb_s])
            op("dve", lambda e: e.tensor_copy(osb_s.ap()[:, 4:8, :], oB), reads=[banks[7]], writes=[osb_s])
            op("dve", lambda e: e.reciprocal(rec_s.ap(), osb_s.ap()[:, :, 64:65]), reads=[osb_s], writes=[rec_s])
            op("dve", lambda e: e.tensor_tensor(yf_s.ap().rearrange("p (h d) -> p h d", h=8), osb_s.ap()[:, :, 0:64], rec_s.ap().to_broadcast([16, 8, 64]), ALU.mult),
               reads=[osb_s, rec_s], writes=[yf_s])
            ytp = bview(6, BF16, [4, 16], off=864)
            for c in range(4):
                op("pe", lambda e, c=c: e.transpose(ytp[:, c, :], yf_s.ap()[:, c * 128:(c + 1) * 128], identb.ap()[0:16, 0:16]), reads=[yf_s, identb], writes=[banks[6]])
            op("dve", lambda e: e.tensor_copy(yfT_sb.ap(), ytp), reads=[banks[6]], writes=[yfT_sb])
            t0 = (NT + st) * 128
            P.dma("sp", yfT_s[:, t0:t0 + 16].rearrange("(c p) t -> p c t", p=128), yfT_sb.ap(), reads=[yfT_sb])
        P.pop()
        P.pop()

        if STOP == 4:
            raise _Stop()
        P.push()
        w_out_t = alloc_weight("w_out", 12, D)
        w_up_t = alloc_weight("w_up", 8, 4096)
        w_dn_t = alloc_weight("w_dn", 32, D)
        for tl, src in ((w_out_t, wo_s), (w_up_t, wu_s), (w_dn_t, wdn_s)):
            for kc, wt in enumerate(tl):
                P.dma("sp", wt.ap(), src[kc * 128:(kc + 1) * 128, :], writes=[wt])
        x3 = [P.sb("x3_%d" % i, [128, D], F32) for i in range(2)]
        yn3 = [P.sb("yn3_%d" % i, [128, D], BF16) for i in range(2)]
        yfT3 = [P.sb("yfT3_%d" % i, [128, 4, 128], BF16) for i in range(2)]
        ynT = P.sb("ynT", [128, 8, 128], BF16)
        junk3 = P.sb("junk3", [128, D], BF16)
        st3 = P.sb("st3", [128, 4], F32)
        h2 = P.sb("h2", [128, D], BF16)
        h2T = P.sb("h2T", [128, 8, 128], BF16)
        rl = [P.sb("rl%d" % i, [128, 4, 128], BF16) for i in range(2)]
        aT = P.sb("aT", [128, 32, 128], BF16)

        NT3 = NT + 1

        def load3(ti, par):
            if ti >= NT3:
                return
            if ti < NT:
                P.dma("sp", x3[par].ap(), x_prompt[ti * 128:(ti + 1) * 128, :], writes=[x3[par]])
                P.dma("sp", yn3[par].ap(), yn_s[ti * 128:(ti + 1) * 128, :], writes=[yn3[par]])
                P.dma("sp", yfT3[par].ap(), yfT_s[:, ti * 128:(ti + 1) * 128].rearrange("(c p) t -> p c t", p=128), writes=[yfT3[par]])
            else:
                op("pool", lambda e: e.memset(x3[par].ap(), 0.0), writes=[x3[par]])
                for st_ in range(NSTREAM):
                    r0 = (NT + st_) * 128
                    P.dma("sp", x3[par].ap()[T * st_:T * (st_ + 1), :], x_sample[st_], writes=[x3[par]])
                    P.dma("sp", yn3[par].ap()[T * st_:T * (st_ + 1), :], yn_s[r0:r0 + T, :], writes=[yn3[par]])
                    P.dma("sp", yfT3[par].ap()[:, :, T * st_:T * (st_ + 1)], yfT_s[:, r0:r0 + T].rearrange("(c p) t -> p c t", p=128), writes=[yfT3[par]])

        load3(0, 0)
        for ti in range(NT3):
            par = ti % 2
            X = x3[par]
            load3(ti + 1, 1 - par)
            tp = bview(0, BF16, [8, 128])
            for c in range(8):
                op("pe", lambda e, c=c: e.transpose(tp[:, c, :], yn3[par].ap()[:, c * 128:(c + 1) * 128], identb.ap()), reads=[yn3[par], identb], writes=[banks[0]])
            op("dve", lambda e: e.tensor_copy(ynT.ap(), tp), reads=[banks[0]], writes=[ynT])
            for cg in range(2):
                ob = bview(1 + cg, F32, [512])
                for kc in range(12):
                    lhs_t = ynT if kc < 8 else yfT3[par]
                    lhs = ynT.ap()[:, kc, :] if kc < 8 else yfT3[par].ap()[:, kc - 8, :]
                    op("pe", lambda e, kc=kc, cg=cg, ob=ob, lhs=lhs: e.matmul(ob, lhs, w_out_t[kc].ap()[:, cg * 512:(cg + 1) * 512], start=(kc == 0), stop=(kc == 11)),
                       reads=[lhs_t, w_out_t[kc]], writes=[banks[1 + cg]])
                op("dve", lambda e, cg=cg, ob=ob: e.tensor_tensor(X.ap()[:, cg * 512:(cg + 1) * 512], X.ap()[:, cg * 512:(cg + 1) * 512], ob, ALU.add),
                   reads=[X, banks[1 + cg]], writes=[X])
            op("act", lambda e: e.activation(out=junk3.ap(), in_=X.ap(), func=AF.Square, accum_out=st3.ap()[:, 0:1]), reads=[X], writes=[st3])
            op("act", lambda e: e.activation(out=st3.ap()[:, 1:2], in_=st3.ap()[:, 0:1], func=AF.Ln, scale=1.0 / D, bias=EPS), reads=[st3], writes=[st3])
            op("act", lambda e: e.activation(out=st3.ap()[:, 2:3], in_=st3.ap()[:, 1:2], func=AF.Exp, scale=-0.5), reads=[st3], writes=[st3])
            op("dve", lambda e: e.tensor_scalar_mul(h2.ap(), X.ap(), st3.ap()[:, 2:3]), reads=[X, st3], writes=[h2])
            for c in range(8):
                op("pe", lambda e, c=c: e.transpose(tp[:, c, :], h2.ap()[:, c * 128:(c + 1) * 128], identb.ap()), reads=[h2, identb], writes=[banks[0]])
            op("dve", lambda e: e.tensor_copy(h2T.ap(), tp), reads=[banks[0]], writes=[h2T])
            for f4 in range(8):
                ub = bview(3 + f4 % 2, F32, [4, 128])
                up_ = f4 % 2
                for j in range(4):
                    fc = f4 * 4 + j
                    for kc in range(8):
                        op("pe", lambda e, kc=kc, fc=fc, j=j, ub=ub: e.matmul(ub[:, j, :], w_up_t[kc].ap()[:, fc * 128:(fc + 1) * 128], h2T.ap()[:, kc, :], start=(kc == 0), stop=(kc == 7)),
                           reads=[w_up_t[kc], h2T], writes=[banks[3 + up_]])
                op("act", lambda e, ub=ub, up_=up_: e.activation(out=rl[up_].ap(), in_=ub, func=AF.Relu), reads=[banks[3 + up_]], writes=[rl[up_]])
                op("pool", lambda e, f4=f4, up_=up_: e.tensor_tensor(aT.ap()[:, f4 * 4:(f4 + 1) * 4, :], rl[up_].ap(), rl[up_].ap(), ALU.mult), reads=[rl[up_]], writes=[aT])
            for cg in range(2):
                db = bview(5 + cg, F32, [512])
                for fc in range(32):
                    op("pe", lambda e, fc=fc, cg=cg, db=db: e.matmul(db, aT.ap()[:, fc, :], w_dn_t[fc].ap()[:, cg * 512:(cg + 1) * 512], start=(fc == 0), stop=(fc == 31)),
                       reads=[aT, w_dn_t[fc]], writes=[banks[5 + cg]])
                op("dve", lambda e, cg=cg, db=db: e.tensor_tensor(X.ap()[:, cg * 512:(cg + 1) * 512], X.ap()[:, cg * 512:(cg + 1) * 512], db, ALU.add),
                   reads=[X, banks[5 + cg]], writes=[X])
            if ti < NT:
                P.dma("pool", y_prompt[ti * 128:(ti + 1) * 128, :], X.ap(), reads=[X])
            else:
                for st_ in range(NSTREAM):
                    P.dma("sp", y_sample[st_], X.ap()[T * st_:T * (st_ + 1), :], reads=[X])
        P.pop()
        P.finish()
      except _Stop:
        P.finish()
    return nc, P


_CACHE = {}


def kernel(x_prompt, x_sample, cache_k, cache_v, cache_logf, state_ssm, state_conv,
           norm1_w, w_in, conv_w, conv_b, dt_bias, A_log, D_skip, ssd_norm_w, f_bias,
           q_norm_w, k_norm_w, w_out, norm2_w, w_up, w_down, n_cores=8):
    f = lambda a: np.ascontiguousarray(np.asarray(a, dtype=np.float32))
    x_prompt = f(x_prompt)
    B, S, _ = x_prompt.shape
    x_sample = f(x_sample)
    DB, T = x_sample.shape[0], x_sample.shape[1]
    PAST = cache_k.shape[2]
    assert B == n_cores and DB % n_cores == 0
    NS = DB // n_cores
    key = (S, PAST, NS, T)
    if key not in _CACHE:
        _CACHE[key] = build_program(S, PAST, NS, T)[0]
    nc = _CACHE[key]
    ck = f(cache_k)[0].reshape(DB, PAST, 512)
    cv = f(cache_v)[0].reshape(DB, PAST, 512)
    clf = f(cache_logf)[0]
    ssm = f(state_ssm)[0].reshape(DB, 1024, 128)
    scv = f(state_conv)[0]
    shared = {"norm1_w": f(norm1_w)[0], "w_in": f(w_in)[0], "conv_w": f(conv_w)[0], "conv_b": f(conv_b)[0],
              "dt_bias": f(dt_bias)[0], "A_log": f(A_log)[0], "D_skip": f(D_skip)[0], "ssd_norm_w": f(ssd_norm_w)[0],
              "f_bias": f(f_bias)[0], "q_norm_w": f(q_norm_w)[0], "k_norm_w": f(k_norm_w)[0], "w_out": f(w_out)[0],
              "norm2_w": f(norm2_w)[0], "w_up": f(w_up)[0], "w_down": f(w_down)[0]}
    in_maps = []
    for c in range(n_cores):
        sl = slice(c * NS, (c + 1) * NS)
        m = {"x_prompt": x_prompt[c], "x_sample": x_sample[sl], "cache_k": ck[sl], "cache_v": cv[sl],
             "cache_logf": clf[sl], "state_ssm": ssm[sl], "state_conv": scv[sl]}
        m.update(shared)
        in_maps.append(m)
    res = run_bass_kernel_spmd(nc, in_maps, core_ids=list(range(n_cores))).results
    global _LAST
    _LAST = res
    cat = lambda n: np.stack([np.asarray(r[n]) for r in res])
    y_prompt = cat("y_prompt")
    y_sample = cat("y_sample").reshape(DB, T, D)
    p_k = cat("p_k").reshape(1, B, S, 8, 64)
    p_v = cat("p_v").reshape(1, B, S, 8, 64)
    p_logf = cat("p_logf").reshape(1, B, S, 8)
    p_ssm = cat("p_ssm").reshape(1, B, 16, 64, 128)
    p_conv = cat("p_conv").reshape(1, B, 3, 1536)
    s_k = cat("s_k").reshape(1, DB, T, 8, 64)
    s_v = cat("s_v").reshape(1, DB, T, 8, 64)
    s_logf = cat("s_logf").reshape(1, DB, T, 8)
    s_ssm = cat("s_ssm").reshape(1, DB, 16, 64, 128)
    s_conv = cat("s_conv").reshape(1, DB, 3, 1536)
    return (y_prompt, y_sample, p_k, p_v, p_logf, p_ssm, p_conv, s_k, s_v, s_logf, s_ssm, s_conv)
```
